# Optimizing a Trainium2 kernel written in Bass

```python
import math
import jax, jax.numpy as jnp
from jax import lax
import numpy as np

D_MODEL = 1024
BATCH = 4
SEQ = 4096
DEPTH = 2

N_META = 16
BLOCK = 128
PAD = BLOCK - N_META

SSM_GROUPS = 16
SSM_GROUP_CH = 16
SSM_WIDTH = SSM_GROUPS * SSM_GROUP_CH
SSM_STATE = 64
SWA_HEADS = 8
SWA_KV_HEADS = 2
HEAD_DIM = 64
WINDOW = 128
SWA_WIDTH = SWA_HEADS * HEAD_DIM
SWA_KV_WIDTH = SWA_KV_HEADS * HEAD_DIM
MLA_HEADS = 4
MLA_Q_RANK = 192
MLA_KV_RANK = 128
MLA_NOPE = 64
MLA_ROPE = 32
MLA_V = 64
MLA_WIDTH = MLA_HEADS * MLA_V
ROPE_THETA = 10000.0

MIX_WIDTH = SSM_WIDTH + SWA_WIDTH + MLA_WIDTH
IN_WIDTH = SSM_WIDTH + SWA_WIDTH + 2 * SWA_KV_WIDTH + MLA_Q_RANK + MLA_KV_RANK + MLA_ROPE
IN_SPLITS = (SSM_WIDTH,
             SSM_WIDTH + SWA_WIDTH,
             SSM_WIDTH + SWA_WIDTH + SWA_KV_WIDTH,
             SSM_WIDTH + SWA_WIDTH + 2 * SWA_KV_WIDTH,
             SSM_WIDTH + SWA_WIDTH + 2 * SWA_KV_WIDTH + MLA_Q_RANK,
             SSM_WIDTH + SWA_WIDTH + 2 * SWA_KV_WIDTH + MLA_Q_RANK + MLA_KV_RANK)
D_FF = 4 * D_MODEL
EPS = 1e-6
NEG = -1e30

kernel_name = 'hymba_s5_swa_mla_hybrid'


def rms_norm(x, g):
    xf = x.astype(jnp.float32)
    y = xf * lax.rsqrt(jnp.mean(xf * xf, axis=-1, keepdims=True) + EPS)
    return (y * g.astype(jnp.float32)).astype(x.dtype)


def rope(x, pos):
    half = x.shape[-1] // 2
    inv_freq = ROPE_THETA ** (-jnp.arange(half, dtype=jnp.float32) / half)
    ang = pos.astype(jnp.float32)[:, None] * inv_freq[None, :]
    cos = jnp.cos(ang)[None, :, None, :]
    sin = jnp.sin(ang)[None, :, None, :]
    xf = x.astype(jnp.float32)
    x1, x2 = xf[..., :half], xf[..., half:]
    return jnp.concatenate([x1 * cos - x2 * sin, x1 * sin + x2 * cos], axis=-1).astype(x.dtype)


def _complex_affine_combine(e1, e2):
    a1r, a1i, b1r, b1i = e1
    a2r, a2i, b2r, b2i = e2
    return (a2r * a1r - a2i * a1i,
            a2r * a1i + a2i * a1r,
            a2r * b1r - a2i * b1i + b2r,
            a2r * b1i + a2i * b1r + b2i)


def s5_mixer(u, valid, a_re, a_im, log_step, b_re, b_im, c_re, c_im, d_skip, w_glu, b_glu):
    f32 = jnp.float32
    bsz, L, _ = u.shape
    uf = jnp.where(valid[None, :, None], u, 0).astype(f32).reshape(bsz, L, SSM_GROUPS, SSM_GROUP_CH)
    step = jnp.exp(log_step.astype(f32))[:, None]
    ar, ai = a_re.astype(f32), a_im.astype(f32)
    mag = jnp.exp(ar * step)
    lb_re, lb_im = mag * jnp.cos(ai * step), mag * jnp.sin(ai * step)
    den = ar * ar + ai * ai
    nr, ni = lb_re - 1.0, lb_im
    coef_re = (nr * ar + ni * ai) / den
    coef_im = (ni * ar - nr * ai) / den
    br, bi = b_re.astype(f32), b_im.astype(f32)
    bb_re = coef_re[..., None] * br - coef_im[..., None] * bi
    bb_im = coef_re[..., None] * bi + coef_im[..., None] * br
    bu_re = jnp.einsum('blgc,gpc->blgp', uf, bb_re)
    bu_im = jnp.einsum('blgc,gpc->blgp', uf, bb_im)
    a_t_re = jnp.broadcast_to(lb_re, bu_re.shape)
    a_t_im = jnp.broadcast_to(lb_im, bu_im.shape)
    _, _, xr, xi = lax.associative_scan(_complex_affine_combine, (a_t_re, a_t_im, bu_re, bu_im), axis=1)
    y = (jnp.einsum('gcp,blgp->blgc', c_re.astype(f32), xr)
         - jnp.einsum('gcp,blgp->blgc', c_im.astype(f32), xi)
         + d_skip.astype(f32).reshape(SSM_GROUPS, SSM_GROUP_CH) * uf)
    z = jax.nn.gelu(y.reshape(bsz, L, SSM_WIDTH))
    out = z * jax.nn.sigmoid(z @ w_glu.astype(f32) + b_glu.astype(f32))
    return out.astype(u.dtype)


def swa_attention(q, k, v, sinks):
    f32 = jnp.float32
    bsz, L = q.shape[:2]
    nb = L // BLOCK
    grp = SWA_HEADS // SWA_KV_HEADS
    qb = q.reshape(bsz, nb, BLOCK, SWA_KV_HEADS, grp, HEAD_DIM)
    kb = k.reshape(bsz, nb, BLOCK, SWA_KV_HEADS, HEAD_DIM)
    vb = v.reshape(bsz, nb, BLOCK, SWA_KV_HEADS, HEAD_DIM)

    def with_prev(t):
        prev = jnp.pad(t, ((0, 0), (1, 0), (0, 0), (0, 0), (0, 0)))[:, :-1]
        return jnp.concatenate([prev, t], axis=2)

    kk, vv = with_prev(kb), with_prev(vb)
    s = jnp.einsum('bnqhgd,bnkhd->bnhgqk', qb, kk).astype(f32) * (HEAD_DIM ** -0.5)
    blk = jnp.arange(nb)
    q_idx = blk[:, None] * BLOCK + jnp.arange(BLOCK)[None, :]
    k_idx = (blk[:, None] - 1) * BLOCK + jnp.arange(2 * BLOCK)[None, :]
    dist = q_idx[:, :, None] - k_idx[:, None, :]
    allowed = (dist >= 0) & (dist < WINDOW) & (k_idx[:, None, :] >= PAD)
    slopes = (2.0 ** (-8.0 * jnp.arange(1, SWA_HEADS + 1, dtype=f32) / SWA_HEADS)).reshape(SWA_KV_HEADS, grp)
    s = s - slopes[None, None, :, :, None, None] * dist.astype(f32)[None, :, None, None]
    s = jnp.where(allowed[None, :, None, None], s, NEG)
    sink = sinks.astype(f32).reshape(SWA_KV_HEADS, grp)[None, None, :, :, None, None]
    m = jnp.maximum(jnp.max(s, axis=-1, keepdims=True), sink)
    p = jnp.exp(s - m)
    p = p / (jnp.sum(p, axis=-1, keepdims=True) + jnp.exp(sink - m))
    o = jnp.einsum('bnhgqk,bnkhd->bnqhgd', p.astype(v.dtype), vv)
    return o.reshape(bsz, L, SWA_WIDTH)


def mla_mixer(cq, ckv, kr, pos, q_norm_g, kv_norm_g, w_uq, w_uk, w_uv):
    f32 = jnp.float32
    bsz, L, _ = cq.shape
    nb = L // BLOCK
    q = (rms_norm(cq, q_norm_g) @ w_uq).reshape(bsz, L, MLA_HEADS, MLA_NOPE + MLA_ROPE)
    q_nope, q_rope = q[..., :MLA_NOPE], rope(q[..., MLA_NOPE:], pos)
    c = rms_norm(ckv, kv_norm_g)
    k_nope = (c @ w_uk).reshape(bsz, L, MLA_HEADS, MLA_NOPE)
    v = (c @ w_uv).reshape(bsz, L, MLA_HEADS, MLA_V)
    k_rope = rope(kr[:, :, None, :], pos)[:, :, 0, :]
    scale = (MLA_NOPE + MLA_ROPE) ** -0.5
    k_idx = jnp.arange(L)
    key_valid = k_idx >= PAD
    qn_b = q_nope.reshape(bsz, nb, BLOCK, MLA_HEADS, MLA_NOPE).swapaxes(0, 1)
    qr_b = q_rope.reshape(bsz, nb, BLOCK, MLA_HEADS, MLA_ROPE).swapaxes(0, 1)

    def one_block(args):
        n, qn, qr = args
        s = (jnp.einsum('bqhd,bkhd->bhqk', qn, k_nope)
             + jnp.einsum('bqhr,bkr->bhqk', qr, k_rope)).astype(f32) * scale
        q_idx = n * BLOCK + jnp.arange(BLOCK)
        allowed = (k_idx[None, :] <= q_idx[:, None]) & key_valid[None, :]
        s = jnp.where(allowed[None, None], s, NEG)
        p = jax.nn.softmax(s, axis=-1)
        return jnp.einsum('bhqk,bkhd->bqhd', p.astype(v.dtype), v)

    o = lax.map(one_block, (jnp.arange(nb), qn_b, qr_b))
    return o.swapaxes(0, 1).reshape(bsz, L, MLA_WIDTH)


def setup_inputs(seed: int = 0) -> dict:
    key = jax.random.key(seed)
    ks = jax.random.split(key, 32)
    f32 = jnp.float32
    nrm = lambda k, shape, s: jax.random.normal(k, shape, f32) * s
    gain = lambda k, shape: 1.0 + 0.02 * jax.random.normal(k, shape, f32)
    n_idx = jnp.arange(SSM_STATE, dtype=f32)
    return {
        'x': nrm(ks[0], (BATCH, SEQ, D_MODEL), 1.0),
        'meta_tokens': nrm(ks[1], (N_META, D_MODEL), 1.0),
        'norm_pre_mix': gain(ks[2], (DEPTH, D_MODEL)),
        'norm_post_mix': gain(ks[3], (DEPTH, D_MODEL)),
        'norm_pre_mlp': gain(ks[4], (DEPTH, D_MODEL)),
        'norm_post_mlp': gain(ks[5], (DEPTH, D_MODEL)),
        'w_in': nrm(ks[6], (DEPTH, D_MODEL, IN_WIDTH), D_MODEL ** -0.5),
        'w_out': nrm(ks[7], (DEPTH, MIX_WIDTH, D_MODEL), MIX_WIDTH ** -0.5),
        'norm_heads': gain(ks[8], (DEPTH, MIX_WIDTH)),
        'ssm_a_re': -0.5 + 0.01 * jax.random.normal(ks[9], (DEPTH, SSM_GROUPS, SSM_STATE), f32),
        'ssm_a_im': math.pi * n_idx[None, None, :] + 0.01 * jax.random.normal(ks[10], (DEPTH, SSM_GROUPS, SSM_STATE), f32),
        'ssm_log_step': jax.random.uniform(ks[11], (DEPTH, SSM_GROUPS), f32, math.log(1e-3), math.log(1e-1)),
        'ssm_b_re': nrm(ks[12], (DEPTH, SSM_GROUPS, SSM_STATE, SSM_GROUP_CH), (2 * SSM_GROUP_CH) ** -0.5),
        'ssm_b_im': nrm(ks[13], (DEPTH, SSM_GROUPS, SSM_STATE, SSM_GROUP_CH), (2 * SSM_GROUP_CH) ** -0.5),
        'ssm_c_re': nrm(ks[14], (DEPTH, SSM_GROUPS, SSM_GROUP_CH, SSM_STATE), SSM_STATE ** -0.5),
        'ssm_c_im': nrm(ks[15], (DEPTH, SSM_GROUPS, SSM_GROUP_CH, SSM_STATE), SSM_STATE ** -0.5),
        'ssm_d': nrm(ks[16], (DEPTH, SSM_WIDTH), 1.0),
        'ssm_w_glu': nrm(ks[17], (DEPTH, SSM_WIDTH, SSM_WIDTH), SSM_WIDTH ** -0.5),
        'ssm_b_glu': nrm(ks[18], (DEPTH, SSM_WIDTH), 0.01),
        'swa_sinks': nrm(ks[19], (DEPTH, SWA_HEADS), 0.5),
        'mla_q_norm': gain(ks[20], (DEPTH, MLA_Q_RANK)),
        'mla_kv_norm': gain(ks[21], (DEPTH, MLA_KV_RANK)),
        'mla_w_uq': nrm(ks[22], (DEPTH, MLA_Q_RANK, MLA_HEADS * (MLA_NOPE + MLA_ROPE)), MLA_Q_RANK ** -0.5),
        'mla_w_uk': nrm(ks[23], (DEPTH, MLA_KV_RANK, MLA_HEADS * MLA_NOPE), MLA_KV_RANK ** -0.5),
        'mla_w_uv': nrm(ks[24], (DEPTH, MLA_KV_RANK, MLA_HEADS * MLA_V), MLA_KV_RANK ** -0.5),
        'w_mlp_up': nrm(ks[25], (DEPTH, D_MODEL, D_FF), D_MODEL ** -0.5),
        'w_mlp_down': nrm(ks[26], (DEPTH, D_FF, D_MODEL), D_FF ** -0.5),
    }


def reference(x, meta_tokens, norm_pre_mix, norm_post_mix, norm_pre_mlp, norm_post_mlp, w_in, w_out,
              norm_heads, ssm_a_re, ssm_a_im, ssm_log_step, ssm_b_re, ssm_b_im, ssm_c_re, ssm_c_im,
              ssm_d, ssm_w_glu, ssm_b_glu, swa_sinks, mla_q_norm, mla_kv_norm, mla_w_uq, mla_w_uk,
              mla_w_uv, w_mlp_up, w_mlp_down):
    bsz, _, d = x.shape
    dt = x.dtype
    meta = jnp.broadcast_to(meta_tokens.astype(dt)[None], (bsz, N_META, d))
    h = jnp.concatenate([jnp.zeros((bsz, PAD, d), dt), meta, x], axis=1)
    L = h.shape[1]
    idx = jnp.arange(L)
    valid = idx >= PAD
    pos = idx - PAD
    for l in range(DEPTH):
        a = rms_norm(h, norm_pre_mix[l])
        proj = a @ w_in[l]
        u, q, k, v, cq, ckv, kr = jnp.split(proj, IN_SPLITS, axis=-1)
        y_ssm = s5_mixer(u, valid, ssm_a_re[l], ssm_a_im[l], ssm_log_step[l], ssm_b_re[l], ssm_b_im[l],
                         ssm_c_re[l], ssm_c_im[l], ssm_d[l], ssm_w_glu[l], ssm_b_glu[l])
        y_swa = swa_attention(q.reshape(bsz, L, SWA_HEADS, HEAD_DIM),
                              k.reshape(bsz, L, SWA_KV_HEADS, HEAD_DIM),
                              v.reshape(bsz, L, SWA_KV_HEADS, HEAD_DIM), swa_sinks[l])
        y_mla = mla_mixer(cq, ckv, kr, pos, mla_q_norm[l], mla_kv_norm[l], mla_w_uq[l], mla_w_uk[l], mla_w_uv[l])
        g = norm_heads[l]
        mixed = jnp.concatenate([
            rms_norm(y_ssm, g[:SSM_WIDTH]),
            rms_norm(y_swa, g[SSM_WIDTH:SSM_WIDTH + SWA_WIDTH]),
            rms_norm(y_mla, g[SSM_WIDTH + SWA_WIDTH:]),
        ], axis=-1)
        h = h + rms_norm(mixed @ w_out[l], norm_post_mix[l])
        a = rms_norm(h, norm_pre_mlp[l])
        f = jnp.square(jax.nn.relu(a @ w_mlp_up[l])) @ w_mlp_down[l]
        h = h + rms_norm(f, norm_post_mlp[l])
    return h[:, PAD + N_META:, :]
```

```python
import math
import numpy as np
import concourse.bass as bass
import concourse.mybir as mybir
from concourse.bass_utils import run_bass_kernel_spmd

F32 = mybir.dt.float32
BF16 = mybir.dt.bfloat16
I32 = mybir.dt.int32
ALU = mybir.AluOpType
AF = mybir.ActivationFunctionType
AX = mybir.AxisListType

ENGS = ("pe", "act", "dve", "pool", "sp")
EPS = 1e-6
PAD = 112


def _cost(eng, n, kind):
    if kind == "dma":
        return 150.0
    if eng == "pe":
        return max(n / 2.05, 70.0) + 5.0
    if eng == "act":
        return 220.0 + 0.8 * n
    if eng == "dve":
        return 100.0 + 0.9 * n
    if eng == "pool":
        return 170.0 + 1.1 * n
    return 100.0


class Tr:
    WINDOW = 5000
    KCAND = 32

    def __init__(self, nc):
        self.nc = nc
        self.ops = []
        self.res = {}
        self.seg = 0
        self.dma_cnt = {}

    def _add(self, eng, kind, fn, r, w, x, key=None, n=None):
        oid = len(self.ops)
        deps = {}
        for name in r:
            st = self.res.setdefault(name, {"w": None, "r": []})
            if st["w"] is not None:
                deps[st["w"]] = True
        for name in list(w) + list(x):
            st = self.res.setdefault(name, {"w": None, "r": []})
            if st["w"] is not None:
                deps.setdefault(st["w"], False)
            for rid in st["r"]:
                deps.setdefault(rid, False)
        if n is None:
            n = getattr(fn, "n", 256) if fn is not None else 0
        self.ops.append({"id": oid, "eng": eng, "kind": kind, "fn": fn, "n": n, "deps": list(deps.items()),
                         "key": key, "seg": self.seg})
        for name in r:
            self.res[name]["r"].append(oid)
        for name in list(w) + list(x):
            st = self.res[name]
            st["w"] = oid
            st["r"] = []
        return oid

    def op(self, eng, fn, r=(), w=(), x=(), n=None):
        self._add(eng, "op", fn, r, w, x, n=n)

    def dma(self, q, key, fn, r=(), w=(), nbytes=65536):
        self._add(q, "dma", fn, r, w, (), key="dma:" + key, n=nbytes)

    def barrier(self):
        self.seg += 1

    def final_wait(self, q, names):
        self._add(q, "wait", None, (), names, ())

    def _schedule(self):
        import heapq
        ops = self.ops
        nseg = self.seg + 1
        order = {e: [] for e in ENGS}
        seg_ops = [[] for _ in range(nseg)]
        for o in ops:
            seg_ops[o["seg"]].append(o["id"])
        finish = {}
        tbase = 0.0
        for sg in range(nseg):
            ids = seg_ops[sg]
            if not ids:
                continue
            idset_lo, idset_hi = ids[0], ids[-1]
            ndeps = {}
            users = {}
            for i in ids:
                c = 0
                for d, _ in ops[i]["deps"]:
                    if d >= idset_lo:
                        c += 1
                        users.setdefault(d, []).append(i)
                ndeps[i] = c
            ready = {e: [] for e in ENGS}
            for i in ids:
                if ndeps[i] == 0:
                    heapq.heappush(ready[ops[i]["eng"]], i)
            free_at = {e: tbase for e in ENGS}
            nsched = 0
            lo_ptr = 0
            scheduled = set()
            total = len(ids)
            while nsched < total:
                while lo_ptr < total and ids[lo_ptr] in scheduled:
                    lo_ptr += 1
                lim = ids[lo_ptr] + self.WINDOW if lo_ptr < total else 1 << 60
                best = None
                for e in ENGS:
                    h = ready[e]
                    if not h:
                        continue
                    cands = heapq.nsmallest(self.KCAND, h)
                    for i in cands:
                        if i > lim and best is not None:
                            break
                        st = free_at[e]
                        for d, _ in ops[i]["deps"]:
                            if d >= idset_lo:
                                lat = 60.0 if ops[d]["eng"] == e and ops[d]["kind"] == "op" else 150.0
                                fd = finish[d] + lat
                                if fd > st:
                                    st = fd
                        pri = (st + (0.0 if i <= lim else 1e9), i)
                        if best is None or pri < best[0]:
                            best = (pri, e, i, st)
                _, e, i, st = best
                ready[e].remove(i)
                heapq.heapify(ready[e])
                o = ops[i]
                c = _cost(e, o["n"], o["kind"])
                if o["kind"] == "dma":
                    free_at[e] = st + (1200.0 if e == "pool" else 150.0)
                    finish[i] = st + 2500.0 + o["n"] / 150.0
                elif o["kind"] == "wait":
                    free_at[e] = st
                    finish[i] = st
                else:
                    free_at[e] = st + c
                    finish[i] = st + c
                order[e].append(i)
                scheduled.add(i)
                nsched += 1
                for u in users.get(i, ()):
                    ndeps[u] -= 1
                    if ndeps[u] == 0:
                        heapq.heappush(ready[ops[u]["eng"]], u)
            tbase = max(max(free_at.values()), max(finish[i] for i in ids))
        return order

    def emit(self):
        nc = self.nc
        ops = self.ops
        order = self._schedule()
        pos = {}
        for e in ENGS:
            for p, i in enumerate(order[e]):
                pos[i] = p
        dma_idx = {}
        kq = {}
        cnt = {}
        for e in ENGS:
            for i in order[e]:
                o = ops[i]
                if o["kind"] == "dma":
                    assert kq.setdefault(o["key"], e) == e, o["key"]
                    k = cnt.get(o["key"], 0)
                    cnt[o["key"]] = k + 1
                    dma_idx[i] = k
        nseg = self.seg + 1
        last_in_seg = [dict() for _ in range(nseg)]
        for e in ENGS:
            for i in order[e]:
                o = ops[i]
                sg = o["seg"]
                if o["kind"] == "op":
                    last_in_seg[sg][e] = i
                elif o["kind"] == "dma":
                    last_in_seg[sg][o["key"]] = i
        waits = {}
        sig = set()
        for e in ENGS:
            wm = {}
            cur_seg = -1
            for i in order[e]:
                o = ops[i]
                need = []
                if o["seg"] != cur_seg:
                    for sg in range(max(cur_seg, 0), o["seg"]):
                        for src, li in last_in_seg[sg].items():
                            if src != e:
                                need.append(li)
                    cur_seg = o["seg"]
                for d, raw in o["deps"]:
                    od = ops[d]
                    if od["kind"] == "wait":
                        continue
                    if od["kind"] == "op" and od["eng"] == e and o["kind"] == "op":
                        if e == "pe" or not raw:
                            continue
                    need.append(d)
                wl = []
                for d in need:
                    od = ops[d]
                    src = od["key"] if od["kind"] == "dma" else od["eng"]
                    p = dma_idx[d] if od["kind"] == "dma" else pos[d]
                    if wm.get(src, -1) >= p:
                        continue
                    wm[src] = p
                    wl.append(d)
                    if od["kind"] == "op":
                        sig.add(d)
                waits[i] = wl
        sems = {e: nc.alloc_semaphore("S_" + e) for e in ENGS}
        for k in cnt:
            sems[k] = nc.alloc_semaphore("D_" + k[4:])
        val = {}
        for e in ENGS:
            c = 0
            for i in order[e]:
                if ops[i]["kind"] == "op" and i in sig:
                    c += 1
                    val[i] = c

        def run(ename, eng):
            for i in order[ename]:
                o = ops[i]
                for d in waits[i]:
                    od = ops[d]
                    if od["kind"] == "dma":
                        eng.wait_ge(sems[od["key"]], 16 * (dma_idx[d] + 1))
                    else:
                        eng.wait_ge(sems[od["eng"]], val[d])
                if o["kind"] == "op":
                    ins = o["fn"](eng)
                    if i in sig:
                        ins.then_inc(sems[ename], 1)
                elif o["kind"] == "dma":
                    o["fn"](eng).then_inc(sems[o["key"]], 16)

        with nc.Block() as block:
            @block.tensor
            def _(e):
                run("pe", e)

            @block.scalar
            def _(e):
                run("act", e)

            @block.vector
            def _(e):
                run("dve", e)

            @block.gpsimd
            def _(e):
                run("pool", e)

            @block.sync
            def _(e):
                run("sp", e)


def build(NB, DEPTH, dbg=None):
    L = NB * 128
    S = L - 128
    nc = bass.Bass("TRN2", target_bir_lowering=False)
    t = Tr(nc)

    def din(name, shape):
        return nc.dram_tensor(name, list(shape), F32, kind="ExternalInput")

    xin = din("xin", [L, 1024])
    out_d = nc.dram_tensor("out", [S, 1024], F32, kind="ExternalOutput")
    hbuf = nc.dram_tensor("hbuf", [L, 1024], F32, kind="Internal")
    dbg_d = nc.dram_tensor("dbg", [NB, 128, 1024], F32, kind="ExternalOutput") if dbg else None
    w_inF_d = din("w_inF", [DEPTH, 1024, 896])
    w_inT_d = din("w_inT", [DEPTH, 1024, 480])
    w_out_d = din("w_out", [DEPTH, 1024, 1024])
    w_up_d = din("w_up", [DEPTH, 1024, 4096])
    w_dn_d = din("w_dn", [DEPTH, 4096, 1024])
    w_uq_d = din("w_uq", [DEPTH, 256, 384])
    w_uk_d = din("w_uk", [DEPTH, 128, 256])
    w_uv_d = din("w_uv", [DEPTH, 128, 256])
    w_glu_d = din("w_glu", [DEPTH, 256, 256])
    gcols_d = din("gcols", [DEPTH, 128, 27])
    gpost_d = din("gpost", [DEPTH, 2, 128, 1024])
    bglu_d = din("bglu", [DEPTH, 128, 256])
    sinks_d = din("sinks", [DEPTH, 128, 8])
    Bblk_d = din("Bblk", [DEPTH, 128, 2048])
    Cblk_d = din("Cblk", [DEPTH, 128, 512])
    Dblk_d = din("Dblk", [DEPTH, 128, 256])
    abc_d = din("abc", [DEPTH, 3, 128, 1024])
    acol_d = din("acol", [DEPTH, 128, 24])
    cbf_d = din("cbf", [128, 128 * 3 + 2048 + 512])
    cf32_d = din("cf32", [128, NB * 32 + 128 + 8])

    arena = nc.alloc_sbuf_tensor("arena", [128, 104000], BF16)[:, :]
    state = {"off": 0}

    def carve(shape, dtype, region=None):
        n = int(np.prod(shape[1:]))
        esz = 4 if dtype in (F32, I32) else 2
        nb2 = (n * esz + 63) // 64 * 32
        reg = region if region is not None else state
        off = reg["off"]
        reg["off"] = off + nb2
        assert reg["off"] <= reg.get("lim", 104000), (shape, reg)
        ap = arena[0:shape[0], off:off + n * esz // 2]
        if esz == 4:
            ap = ap.bitcast(dtype)
        if len(shape) == 3:
            ap = ap.rearrange("p (a b) -> p a b", b=shape[2])
        elif len(shape) == 4:
            ap = ap.rearrange("p (a b c) -> p a b c", b=shape[2], c=shape[3])
        return ap

    ident = carve([128, 128], BF16)
    tri = carve([128, 128], BF16)
    ones = carve([128, 128], BF16)
    eswa = carve([128, 2, 1024], BF16)
    mask4 = carve([128, 512], BF16)
    ntri = carve([128, 128], BF16)
    ropec = carve([128, NB, 16], F32)
    ropes = carve([128, NB, 16], F32)
    tvals = carve([128, 128], F32)
    misc = carve([128, 8], F32)
    gcols = carve([128, 27], F32)
    rtm = carve([128, 4, 2, 4], F32)
    rst = {"i": 0}
    gpost = carve([128, 1024], F32)
    hs = [carve([128, 1024], F32) for _ in range(2)]
    a_bf = carve([128, 1024], BF16)
    aT = carve([128, 8, 128], BF16)
    sqj = carve([128, 1024], BF16)
    stats = [carve([128, 16], F32) for _ in range(2)]
    hnew = [carve([128, 1024], F32)] * 2
    union0 = state["off"]

    M = {"off": union0, "lim": 104000}
    w_inF = carve([128, 8, 896], BF16, M)
    w_inT = carve([128, 8, 480], BF16, M)
    w_out = carve([128, 8, 1024], BF16, M)
    w_uq = carve([128, 2, 384], BF16, M)
    w_uk = carve([128, 256], BF16, M)
    w_uv = carve([128, 256], BF16, M)
    w_glu = carve([128, 2, 256], BF16, M)
    Bblk = carve([128, 4, 512], BF16, M)
    Cblk = carve([128, 2, 8, 32], BF16, M)
    nCblk = carve([128, 2, 8, 32], BF16, M)
    Dblk = carve([128, 2, 128], BF16, M)
    bglu = carve([128, 256], F32, M)
    esink = carve([128, 8], F32, M)
    Wre = carve([128, 1024], F32, M)
    Wim = carve([128, 1024], F32, M)
    Ere = carve([128, 8, 128], F32, M)
    Eim = carve([128, 8, 128], F32, M)
    acol = carve([128, 24], F32, M)
    Xst = carve([128, 2, 8], F32, M)
    cst = carve([128, 2, 8], F32, M)
    sm8 = carve([128, 8, 8], F32, M)
    PP = carve([128, 4, 1024], BF16, M)
    QQ = PP
    zc = carve([128, 2, 1024], F32, M)
    Lk = max(L, 2816)
    KT = carve([128, 4, Lk], BF16, M)
    Tal = {"off": M["off"] - (4 * Lk * 2 + 63) // 64 * 32, "lim": M["off"]}
    T1 = carve([128, 1024], F32, Tal)
    T2 = carve([128, 1024], F32, Tal)
    T3 = carve([128, 1024], F32, Tal)
    T4 = carve([128, 1024], F32, Tal)
    TI = carve([128, 1024], I32, Tal)
    uT = carve([128, 2, 128], BF16, M)
    qT = carve([128, 4, 128], BF16, M)
    kTs = [carve([128, 128], BF16, M) for _ in range(2)]
    vaugs = [carve([128, 2, 65], BF16, M) for _ in range(2)]
    cqkvs = [carve([128, 320], BF16, M) for _ in range(2)]
    krfs = [carve([128, 32], F32, M) for _ in range(2)]
    krb = carve([128, 96], BF16, M)
    dq = carve([128, 128], BF16, M)
    dkv = carve([128, 128], BF16, M)
    cqT = carve([128, 2, 128], BF16, M)
    cT = carve([128, 128], BF16, M)
    qf = carve([128, 4, 96], F32, M)
    qb = carve([128, 4, 96], BF16, M)
    rtmp = carve([128, 4, 4, 16], F32, M)
    QTs = [carve([128, 4, 128], BF16, M) for _ in range(2)]
    Vaug = carve([128, NB, 4, 65], BF16, M)
    pTs = [carve([128, 512], BF16, M) for _ in range(3)]
    pS = [carve([128, 512], BF16, M) for _ in range(4)]
    ysb = carve([128, 256], F32, M)
    g1 = carve([128, 256], F32, M)
    g2 = carve([128, 256], F32, M)
    zf = carve([128, 256], F32, M)
    zb = carve([128, 256], BF16, M)
    zT = carve([128, 2, 128], BF16, M)
    yssms = [carve([128, 256], F32, M) for _ in range(2)]
    oswas = [carve([128, 8, 64], F32, M) for _ in range(2)]
    omlas = [carve([128, 4, 64], F32, M) for _ in range(2)]
    dens = [carve([128, 8], F32, M) for _ in range(2)]
    mixeds = [carve([128, 1024], BF16, M) for _ in range(2)]
    mixTs = [carve([128, 8, 128], BF16, M) for _ in range(2)]

    Fr = {"off": union0, "lim": 104000}
    w_up = carve([128, 8, 4096], BF16, Fr)
    w_dn = carve([128, 32, 1024], BF16, Fr)
    hT = carve([128, 32, 128], BF16, Fr)
    rl = [carve([128, 512], F32, Fr) for _ in range(2)]
    stg = [carve([128, 2048], F32, Fr) for _ in range(3)]

    print("SBUF carve: persistent", union0 * 2, "M end", M["off"] * 2, "F end", Fr["off"] * 2)
    banks = [nc.alloc_psum_tensor("pb%d" % i, [128, 512], F32)[:, :] for i in range(8)]
    bstate = {"A": 0, "E": 0, "L": 0}
    bgroups = {"A": list(range(8)), "E": [0, 1, 2, 3], "L": [4, 5, 6, 7]}

    def nb(grp="A"):
        g = bgroups[grp]
        i = g[bstate[grp] % len(g)]
        bstate[grp] += 1
        return banks[i], "pb%d" % i

    V = lambda fn, r=(), w=(), x=(): t.op("dve", fn, r, w, x)
    A = lambda fn, r=(), w=(), x=(): t.op("act", fn, r, w, x)
    P = lambda fn, r=(), w=(), x=(): t.op("pe", fn, r, w, x)
    G = lambda fn, r=(), w=(), x=(): t.op("pool", fn, r, w, x)

    def fsz(ap):
        return int(np.prod(ap.shape[1:]))

    def hint(f, ap):
        f.n = fsz(ap)
        return f

    def tt(out, a, b, op):
        return hint(lambda e: e.tensor_tensor(out=out, in0=a, in1=b, op=op), out)

    def ts(out, a, s1, op0, s2=None, op1=None):
        if op1 is None:
            return hint(lambda e: e.tensor_scalar(out=out, in0=a, scalar1=s1, scalar2=None, op0=op0), out)
        return hint(lambda e: e.tensor_scalar(out=out, in0=a, scalar1=s1, scalar2=s2, op0=op0, op1=op1), out)

    def act(out, in_, func, scale=1.0, accum=None):
        if accum is None:
            return hint(lambda e: e.activation(out=out, in_=in_, func=func, scale=scale), out)
        return hint(lambda e: e.activation(out=out, in_=in_, func=func, scale=scale, accum_out=accum), out)

    def mm(out, lhsT, rhs, start, stop):
        return hint(lambda e: e.matmul(out, lhsT=lhsT, rhs=rhs, start=start, stop=stop), out)

    def tp(out, in_):
        return hint(lambda e: e.transpose(out, in_, ident[0:in_.shape[0], 0:in_.shape[0]]), out)

    t.dma("pool", "c_ident", lambda e: e.dma_start(out=ident, in_=cbf_d[:, 0:128]), w=["ident"])
    t.dma("pool", "c_tri", lambda e: e.dma_start(out=tri, in_=cbf_d[:, 128:256]), w=["tri"])
    t.dma("pool", "c_ones", lambda e: e.dma_start(out=ones, in_=cbf_d[:, 256:384]), w=["ones"])
    t.dma("pool", "c_eswa", lambda e: e.dma_start(out=eswa.rearrange("p a b -> p (a b)"), in_=cbf_d[:, 384:384 + 2048]), w=["eswa"])
    t.dma("pool", "c_mask4", lambda e: e.dma_start(out=mask4, in_=cbf_d[:, 2432:2944]), w=["mask4"])
    t.dma("sp", "c_ropec", lambda e: e.dma_start(out=ropec.rearrange("p a b -> p (a b)"), in_=cf32_d[:, 0:NB * 16]), w=["ropec"])
    t.dma("sp", "c_ropes", lambda e: e.dma_start(out=ropes.rearrange("p a b -> p (a b)"), in_=cf32_d[:, NB * 16:NB * 32]), w=["ropes"])
    t.dma("sp", "c_tvals", lambda e: e.dma_start(out=tvals, in_=cf32_d[:, NB * 32:NB * 32 + 128]), w=["tvals"])
    t.dma("sp", "c_misc", lambda e: e.dma_start(out=misc, in_=cf32_d[:, NB * 32 + 128:NB * 32 + 136]), w=["misc"])
    V(ts(ntri, tri, -1.0, ALU.mult), r=["tri"], w=["ntri"])
    tcol, ntcol, valid0, nhalf = misc[:, 0:1], misc[:, 1:2], misc[:, 2:3], misc[:, 3:4]
    TWO_PI = 2.0 * math.pi

    def rstd_pool(ss_ap, ncol, invs, out_ap, rname, wname):
        for c, inv in enumerate(invs):
            G(ts(out_ap[:, c:c + 1], ss_ap[:, c:c + 1], inv, ALU.mult, EPS, ALU.add), r=[rname], w=[wname])
        G(tt(out_ap[:, 0:ncol], out_ap[:, 0:ncol], nhalf.to_broadcast([128, ncol]), ALU.pow), r=[wname, "misc"], w=[wname])

    def sincos(outs, outc, ang, r, wn):
        for dst, shift in ((outs, 0.0), (outc, 0.5 * math.pi)):
            V(ts(T3, ang, 1.0 / TWO_PI, ALU.mult, shift / TWO_PI, ALU.add), r=r, w=["T3"])
            V(lambda e: e.tensor_copy(out=TI, in_=T3), r=["T3"], w=["TI"])
            V(lambda e: e.tensor_copy(out=T4, in_=TI), r=["TI"], w=["T4"])
            V(tt(T3, T3, T4, ALU.subtract), r=["T3", "T4"], w=["T3"])
            V(ts(T3, T3, TWO_PI, ALU.mult, math.pi, ALU.min), r=["T3"], w=["T3"])
            V(ts(T3, T3, -math.pi, ALU.max), r=["T3"], w=["T3"])
            A(act(dst, T3, AF.Sin), r=["T3"], w=[wn])

    def phaseM(l):
        def wload(key, dst, src, g0=None, nk=1):
            t.dma("pool", key, lambda e: e.dma_start(out=dst, in_=src), w=[key])
            if g0 is not None:
                for k in range(nk):
                    d = dst[:, k] if nk > 1 or len(dst.shape) == 3 else dst
                    V(ts(d, d, gcols[:, g0 + k:g0 + k + 1], ALU.mult), r=["gcols"], w=[key])
        t.dma("sp", "gcols", lambda e: e.dma_start(out=gcols, in_=gcols_d[l]), w=["gcols"])
        t.dma("sp", "gpost", lambda e: e.dma_start(out=gpost, in_=gpost_d[l, 0]), w=["gpost"])
        t.dma("sp", "bglu", lambda e: e.dma_start(out=bglu, in_=bglu_d[l]), w=["bglu"])
        t.dma("sp", "esink", lambda e: e.dma_start(out=esink, in_=sinks_d[l]), w=["esink"])
        t.dma("sp", "acol", lambda e: e.dma_start(out=acol, in_=acol_d[l]), w=["acol"])
        A(act(esink, esink, AF.Exp), r=["esink"], w=["esink"])
        wload("w_inF", w_inF, w_inF_d[l].rearrange("(k p) c -> p k c", p=128), 0, 8)
        wload("w_inT", w_inT, w_inT_d[l].rearrange("(k p) c -> p k c", p=128), 0, 8)
        wload("w_out", w_out, w_out_d[l].rearrange("(k p) c -> p k c", p=128), 8, 8)
        wload("w_uq", w_uq, w_uq_d[l].rearrange("(k p) c -> p k c", p=128), 24, 2)
        wload("w_uk", w_uk, w_uk_d[l], None)
        V(ts(w_uk, w_uk, gcols[:, 26:27], ALU.mult), r=["gcols"], w=["w_uk"])
        wload("w_uv", w_uv, w_uv_d[l], None)
        V(ts(w_uv, w_uv, gcols[:, 26:27], ALU.mult), r=["gcols"], w=["w_uv"])
        wload("w_glu", w_glu, w_glu_d[l].rearrange("(k p) c -> p k c", p=128), None)
        wload("Bblk", Bblk.rearrange("p a b -> p (a b)"), Bblk_d[l], None)
        wload("Cblk", Cblk.rearrange("p a b c -> p (a b c)"), Cblk_d[l], None)
        V(ts(nCblk.rearrange("p a b c -> p (a b c)"), Cblk.rearrange("p a b c -> p (a b c)"), -1.0, ALU.mult), r=["Cblk"], w=["nCblk"])
        wload("Dblk", Dblk.rearrange("p a b -> p (a b)"), Dblk_d[l], None)
        t.dma("sp", "T1", lambda e: e.dma_start(out=T1, in_=abc_d[l, 0]), w=["T1"])
        t.dma("sp", "T2", lambda e: e.dma_start(out=T2, in_=abc_d[l, 1]), w=["T2"])
        t.dma("sp", "Wre", lambda e: e.dma_start(out=Wre, in_=abc_d[l, 2]), w=["Wre"])
        A(act(Wre, Wre, AF.Exp), r=["Wre"], w=["Wre"])
        zr, zi = zc[:, 0], zc[:, 1]
        V(tt(zr, T1, Wre, ALU.mult), r=["T1", "Wre"], w=["zc"])
        V(tt(zi, T2, Wre, ALU.mult), r=["T2", "Wre"], w=["zc"])
        PPf = PP.bitcast(F32).rearrange("p a b -> p (a b)")
        lbr, lbi = PPf[:, 0:1024], PPf[:, 1024:2048]
        sincos(lbi, lbr, zi, ["zc"], "PP")
        A(act(Wim, zr, AF.Exp), r=["zc"], w=["Wim"])
        V(tt(lbr, lbr, Wim, ALU.mult), r=["PP", "Wim"], w=["PP"])
        V(tt(lbi, lbi, Wim, ALU.mult), r=["PP", "Wim"], w=["PP"])
        cre, cim = Ere.rearrange("p k t -> p (k t)"), Eim.rearrange("p k t -> p (k t)")
        V(ts(lbr, lbr, -1.0, ALU.add), r=["PP"], w=["PP"])
        V(tt(T3, T1, T1, ALU.mult), r=["T1"], w=["T3"])
        V(tt(T4, T2, T2, ALU.mult), r=["T2"], w=["T4"])
        V(tt(T3, T3, T4, ALU.add), r=["T3", "T4"], w=["T3"])
        V(lambda e: e.reciprocal(out=T3, in_=T3), r=["T3"], w=["T3"])
        V(tt(cre, lbr, T1, ALU.mult), r=["PP", "T1"], w=["E", "PP"])
        V(tt(T4, lbi, T2, ALU.mult), r=["PP", "T2"], w=["T4"])
        V(tt(cre, cre, T4, ALU.add), r=["PP", "T4"], w=["E", "PP"])
        V(tt(cre, cre, T3, ALU.mult), r=["PP", "T3"], w=["E", "PP"])
        V(tt(cim, lbi, T1, ALU.mult), r=["PP", "T1"], w=["E", "PP"])
        V(tt(T4, lbr, T2, ALU.mult), r=["PP", "T2"], w=["T4"])
        V(tt(cim, cim, T4, ALU.subtract), r=["PP", "T4"], w=["E", "PP"])
        V(tt(cim, cim, T3, ALU.mult), r=["PP", "T3"], w=["E", "PP"])
        A(lambda e: e.activation(out=T1, in_=zr, func=AF.Exp, scale=ntcol), r=["zc", "misc"], w=["T1"])
        V(ts(T2, zi, ntcol, ALU.mult), r=["zc", "misc"], w=["T2"])
        sincos(lbi, lbr, T2, ["T2"], "PP")
        V(tt(lbr, lbr, T1, ALU.mult), r=["PP", "T1"], w=["PP"])
        V(tt(lbi, lbi, T1, ALU.mult), r=["PP", "T1"], w=["PP"])
        V(tt(Wre, cre, lbr, ALU.mult), r=["PP", "PP"], w=["E", "Wre"])
        V(tt(T4, cim, lbi, ALU.mult), r=["PP", "PP"], w=["E", "T4"])
        V(tt(Wre, Wre, T4, ALU.subtract), r=["Wre", "T4"], w=["Wre"])
        V(tt(Wim, cre, lbi, ALU.mult), r=["PP", "PP"], w=["E", "Wim"])
        V(tt(T4, cim, lbr, ALU.mult), r=["PP", "PP"], w=["E", "T4"])
        V(tt(Wim, Wim, T4, ALU.add), r=["Wim", "T4"], w=["Wim"])
        A(act(acol[:, 16:24], acol[:, 16:24], AF.Exp), r=["acol"], w=["acol"])
        V(tt(acol[:, 0:8], acol[:, 0:8], acol[:, 16:24], ALU.mult), r=["acol"], w=["acol"])
        V(tt(acol[:, 8:16], acol[:, 8:16], acol[:, 16:24], ALU.mult), r=["acol"], w=["acol"])
        T1v = T1.rearrange("p (k t) -> p k t", t=128)
        T2v = T2.rearrange("p (k t) -> p k t", t=128)
        for k in range(8):
            A(lambda e, k=k: e.activation(out=T1v[:, k], in_=tvals, func=AF.Exp, scale=acol[:, k:k + 1]), r=["tvals", "acol"], w=["T1"])
            V(ts(T2v[:, k], tvals, acol[:, 8 + k:9 + k], ALU.mult), r=["tvals", "acol"], w=["T2"])
        Ere2, Eim2 = Ere.rearrange("p k t -> p (k t)"), Eim.rearrange("p k t -> p (k t)")
        sincos(Eim2, Ere2, T2, ["T2"], "E")
        V(tt(Ere2, Ere2, T1, ALU.mult), r=["E", "T1"], w=["E"])
        V(tt(Eim2, Eim2, T1, ALU.mult), r=["E", "T1"], w=["E"])
        V(lambda e: e.memset(Xst.rearrange("p a b -> p (a b)"), 0.0), w=["Xst"])
        V(lambda e: e.memset(krb, 0.0), w=["krb"])
        V(lambda e: e.memset(Vaug[:, :, :, 64:65].rearrange("p n h o -> p (n h o)"), 1.0), w=["Vones"])
        V(lambda e: e.tensor_copy(out=Vaug[:, 0, :, 64:65].rearrange("p h o -> p (h o)"), in_=valid0.to_broadcast([128, 4])), r=["misc", "Vones"], w=["Vones"])

        t.barrier()
        for n in range(NB):
            blockM(l, n)

    def load_h(l, n, first_phase):
        h = hs[n % 2]
        hn = "h%d" % (n % 2)
        src = xin if (l == 0 and first_phase) else hbuf
        t.dma("sp", hn, lambda e: e.dma_start(out=h, in_=src[n * 128:(n + 1) * 128, :]), r=["hbufrow%d" % n], w=[hn])
        return h, hn

    def prenorm(h, hn, n, grp="A", g0=None):
        stat = stats[n % 2]
        sfx = "_%d" % (n % 2)
        A(act(sqj, h, AF.Square, accum=stat[:, 0:1]), r=[hn], w=["stat0" + sfx])
        rstd_pool(stat[:, 0:1], 1, [1.0 / 1024], stat[:, 1:2], "stat0" + sfx, "stat1" + sfx)
        V(ts(a_bf, h, stat[:, 1:2], ALU.mult), r=[hn, "stat1" + sfx], w=["a_bf"])
        bk, bn = nb(grp)
        bkb = bk.bitcast(BF16)
        for k in range(8):
            P(tp(bkb[:, k * 128:(k + 1) * 128], a_bf[:, k * 128:(k + 1) * 128]), r=["a_bf", "ident"], x=[bn])
        if g0 is None:
            A(act(aT.rearrange("p k t -> p (k t)"), bkb, AF.Copy), x=[bn], w=["aT"])
        else:
            for k in range(8):
                A(hint(lambda e, k=k: e.activation(out=aT[:, k, :], in_=bkb[:, k * 128:(k + 1) * 128], func=AF.Copy, scale=gcols[:, g0 + k:g0 + k + 1]), aT[:, k, :]), r=["gcols"], x=[bn], w=["aT"])

    def postnorm_store(l, n, b0, b0n, b1, b1n, h, hn, dst_rows):
        stat = stats[n % 2]
        sfx = "_%d" % (n % 2)
        A(act(sqj[:, 0:512], b0, AF.Square, accum=stat[:, 2:3]), x=[b0n], w=["stat2" + sfx])
        A(act(sqj[:, 512:1024], b1, AF.Square, accum=stat[:, 3:4]), x=[b1n], w=["stat2" + sfx])
        G(tt(stat[:, 4:5], stat[:, 2:3], stat[:, 3:4], ALU.add), r=["stat2" + sfx], w=["stat4" + sfx])
        rstd_pool(stat[:, 4:5], 1, [1.0 / 1024], stat[:, 5:6], "stat4" + sfx, "stat5" + sfx)
        ho = hnew[0]
        hon = "hnew0"
        V(lambda e: e.scalar_tensor_tensor(out=ho[:, 0:512], in0=b0, scalar=stat[:, 5:6], in1=gpost[:, 0:512], op0=ALU.mult, op1=ALU.mult), r=["stat5" + sfx, "gpost"], x=[b0n], w=[hon])
        V(lambda e: e.scalar_tensor_tensor(out=ho[:, 512:1024], in0=b1, scalar=stat[:, 5:6], in1=gpost[:, 512:1024], op0=ALU.mult, op1=ALU.mult), r=["stat5" + sfx, "gpost"], x=[b1n], w=[hon])
        G(tt(ho, ho, h, ALU.add), r=[hon, hn], w=[hon])
        dst = dst_rows
        t.dma("sp", hon, lambda e: e.dma_start(out=dst, in_=ho), r=[hon], w=["hbufrow%d" % n])

    def blockM(l, n):
        par = n % 2
        sfx = "_%d" % par
        stat, QT, yssm, oswa, omla, den, mixed, mixT, cqkv, krf = (stats[par], QTs[par], yssms[par], oswas[par], omlas[par],
                                                                     dens[par], mixeds[par], mixTs[par], cqkvs[par], krfs[par])
        h, hn = load_h(l, n, True)
        prenorm(h, hn, n, "E")
        sl = n % 2
        kT, vaug = kTs[sl], vaugs[sl]
        kTn, vaugn = "kT%d" % sl, "vaug%d" % sl
        bA, bAn = nb("E")
        for k in range(8):
            P(mm(bA[:, 0:480], aT[:, k, :], w_inT[:, k, :], k == 0, k == 7), r=["aT", "w_inT"], x=[bAn])
        bB, bBn = nb("E")
        bC, bCn = nb("E")
        for m in range(7):
            dstb = bB[:, m * 128:(m + 1) * 128] if m < 4 else bC[:, (m - 4) * 128:(m - 3) * 128]
            for k in range(8):
                P(mm(dstb, w_inF[:, k, m * 128:(m + 1) * 128], aT[:, k, :], k == 0, k == 7), r=["aT", "w_inF"], x=[bBn if m < 4 else bCn])
        V(lambda e: e.tensor_copy(out=vaug[:, :, 0:64], in_=bA[:, 0:128].rearrange("p (g d) -> p g d", g=2)), x=[bAn], w=[vaugn])
        V(lambda e: e.tensor_copy(out=vaug[:, :, 64:65].rearrange("p g o -> p (g o)"), in_=(valid0 if n == 0 else ones[:, 0:1]).to_broadcast([128, 2])), r=["misc", "ones"], w=[vaugn])
        A(act(sqj[:, 0:192], bA[:, 128:320], AF.Square, accum=stat[:, 6:7]), x=[bAn], w=["stat6" + sfx])
        A(act(sqj[:, 192:320], bA[:, 320:448], AF.Square, accum=stat[:, 7:8]), x=[bAn], w=["stat6" + sfx])
        V(lambda e: e.tensor_copy(out=cqkv, in_=bA[:, 128:448]), x=[bAn], w=["cqkv" + sfx])
        V(lambda e: e.tensor_copy(out=krf, in_=bA[:, 448:480]), x=[bAn], w=["krf" + sfx])
        V(lambda e: e.tensor_copy(out=qT.rearrange("p k t -> p (k t)"), in_=bB), x=[bBn], w=["qT"])
        A(act(uT.rearrange("p k t -> p (k t)"), bC[:, 0:256], AF.Copy), x=[bCn], w=["uT"])
        A(act(kT, bC[:, 256:384], AF.Copy), x=[bCn], w=[kTn])
        rstd_pool(stat[:, 6:8], 2, [1.0 / 192, 1.0 / 128], stat[:, 8:10], "stat6" + sfx, "stat8" + sfx)

        V(ts(dq, ident, stat[:, 8:9], ALU.mult), r=["ident", "stat8" + sfx], w=["dq"])
        V(ts(dkv, ident, stat[:, 9:10], ALU.mult), r=["ident", "stat8" + sfx], w=["dkv"])
        b1, b1n = nb("E")
        P(mm(b1[:, 0:128], cqkv[:, 0:128], dq, True, True), r=["cqkv" + sfx, "dq"], x=[b1n])
        P(mm(b1[0:64, 128:256], cqkv[:, 128:192], dq, True, True), r=["cqkv" + sfx, "dq"], x=[b1n])
        P(mm(b1[:, 256:384], cqkv[:, 192:320], dkv, True, True), r=["cqkv" + sfx, "dkv"], x=[b1n])
        A(act(cqT[:, 0, :], b1[:, 0:128], AF.Copy), x=[b1n], w=["cqT"])
        A(act(cqT[0:64, 1, :], b1[0:64, 128:256], AF.Copy), x=[b1n], w=["cqT"])
        A(act(cT, b1[:, 256:384], AF.Copy), x=[b1n], w=["cT"])
        b2, b2n = nb("E")
        P(mm(b2[:, 0:384], cqT[:, 0, :], w_uq[:, 0, :], True, False), r=["cqT", "w_uq"], x=[b2n])
        P(mm(b2[:, 0:384], cqT[0:64, 1, :], w_uq[0:64, 1, :], False, True), r=["cqT", "w_uq"], x=[b2n])
        b3, b3n = nb("E")
        for hh in range(4):
            P(mm(b3[0:64, hh * 128:(hh + 1) * 128], w_uk[:, hh * 64:(hh + 1) * 64], cT, True, True), r=["cT", "w_uk"], x=[b3n])
        b4, b4n = nb("E")
        P(mm(b4[:, 0:256], cT, w_uv, True, True), r=["cT", "w_uv"], x=[b4n])
        KTn, Vn = "KT%d" % n, "V%d" % n
        ncol = slice(n * 128, (n + 1) * 128)
        V(lambda e: e.tensor_copy(out=KT[0:64, :, ncol], in_=b3[0:64, :].rearrange("p (h t) -> p h t", t=128)), x=[b3n], w=[KTn])
        V(lambda e: e.tensor_copy(out=Vaug[:, n, :, 0:64], in_=b4[:, 0:256].rearrange("p (h d) -> p h d", d=64)), x=[b4n], w=[Vn])
        A(act(qf.rearrange("p h d -> p (h d)"), b2[:, 0:384], AF.Copy), x=[b2n], w=["qf"])
        cb = ropec[:, n:n + 1, :].to_broadcast([128, 4, 16])
        sb_ = ropes[:, n:n + 1, :].to_broadcast([128, 4, 16])
        x1, x2 = qf[:, :, 64:80], qf[:, :, 80:96]
        V(tt(rtmp[:, 0], x1, cb, ALU.mult), r=["qf", "ropec"], w=["rtmp"])
        V(tt(rtmp[:, 1], x2, sb_, ALU.mult), r=["qf", "ropes"], w=["rtmp"])
        V(tt(rtmp[:, 2], x1, sb_, ALU.mult), r=["qf", "ropes"], w=["rtmp"])
        V(tt(rtmp[:, 3], x2, cb, ALU.mult), r=["qf", "ropec"], w=["rtmp"])
        V(tt(qb[:, :, 64:80], rtmp[:, 0], rtmp[:, 1], ALU.subtract), r=["rtmp"], w=["qb"])
        V(tt(qb[:, :, 80:96], rtmp[:, 2], rtmp[:, 3], ALU.add), r=["rtmp"], w=["qb"])
        V(lambda e: e.tensor_copy(out=qb[:, :, 0:64], in_=qf[:, :, 0:64]), r=["qf"], w=["qb"])
        b5, b5n = nb("E")
        b5b = b5.bitcast(BF16)
        for hh in range(4):
            P(tp(b5b[0:96, hh * 128:(hh + 1) * 128], qb[:, hh, :]), r=["qb", "ident"], x=[b5n])
        A(act(QT[0:96].rearrange("p h t -> p (h t)"), b5b[0:96, 0:512], AF.Copy), x=[b5n], w=["QT" + sfx])
        c1, s1 = ropec[:, n, :], ropes[:, n, :]
        k1, k2 = krf[:, 0:16], krf[:, 16:32]
        rt = rtmp.rearrange("p a b c -> p (a b c)")
        V(tt(rt[:, 0:16], k1, c1, ALU.mult), r=["krf" + sfx, "ropec"], w=["rtmp"])
        V(tt(rt[:, 16:32], k2, s1, ALU.mult), r=["krf" + sfx, "ropes"], w=["rtmp"])
        V(tt(rt[:, 32:48], k1, s1, ALU.mult), r=["krf" + sfx, "ropes"], w=["rtmp"])
        V(tt(rt[:, 48:64], k2, c1, ALU.mult), r=["krf" + sfx, "ropec"], w=["rtmp"])
        V(tt(krb[:, 64:80], rt[:, 0:16], rt[:, 16:32], ALU.subtract), r=["rtmp"], w=["krb"])
        V(tt(krb[:, 80:96], rt[:, 32:48], rt[:, 48:64], ALU.add), r=["rtmp"], w=["krb"])
        b6, b6n = nb("E")
        b6b = b6.bitcast(BF16)
        P(tp(b6b[0:96, 0:128], krb), r=["krb", "ident"], x=[b6n])
        A(lambda e: e.activation(out=KT[64:96, :, ncol], in_=b6b[64:96, 0:128].unsqueeze(1).to_broadcast([32, 4, 128]), func=AF.Copy), x=[b6n], w=[KTn])
        bo, bon = nb("L")
        bov = bo[:, 0:260].rearrange("p (h d) -> p h d", d=65)
        scale = 96.0 ** -0.5
        for j in range(n + 1):
            bs, bsn = nb("L")
            if bsn == bon:
                bs, bsn = nb("L")
            for hh in range(4):
                P(mm(bs[:, hh * 128:(hh + 1) * 128], KT[0:96, hh, j * 128:(j + 1) * 128], QT[0:96, hh, :], True, True), r=["KT%d" % j, "QT" + sfx], x=[bsn])
            p_ = pTs[j % 3]
            pn = "pT%d" % (j % 3)
            A(act(p_, bs, AF.Exp, scale=scale), x=[bsn], w=[pn])
            if j == n:
                V(tt(p_, p_, mask4, ALU.mult), r=["mask4"], w=[pn])
            for hh in range(4):
                P(mm(bov[:, hh, :], p_[:, hh * 128:(hh + 1) * 128], Vaug[:, j, hh, :], j == 0 and hh == 0, j == n and hh == 3), r=[pn, "V%d" % j, "Vones"], x=[bon])
        V(ts(den[:, 4:8], bov[:, :, 64], 1e-30, ALU.add), x=[bon], w=["den2" + sfx])
        V(lambda e: e.reciprocal(out=den[:, 4:8], in_=den[:, 4:8]), r=["den2" + sfx], w=["den2" + sfx])
        V(tt(omla, bov[:, :, 0:64], den[:, 4:8].unsqueeze(2).to_broadcast([128, 4, 64]), ALU.mult), r=["den2" + sfx], x=[bon], w=["omla" + sfx])
        A(act(sqj[:, 0:256], omla.rearrange("p h d -> p (h d)"), AF.Square, accum=stat[:, 12:13]), r=["omla" + sfx], w=["stat10" + sfx])

        bu = {}
        for j in range(2):
            for ri in range(2):
                bk, bn = nb("E")
                bu[(j, ri)] = (bk, bn)
                P(mm(bk, uT[:, j, :], Bblk[:, ri * 2 + j, :], True, True), r=["uT", "Bblk"], x=[bn])
        for j in range(2):
            sl_ = slice(j * 512, (j + 1) * 512)
            (br, brn), (bi, bin_) = bu[(j, 0)], bu[(j, 1)]
            V(tt(PP[:, 0, sl_], br, Wre[:, sl_], ALU.mult), r=["Wre"], x=[brn], w=["PP"])
            V(tt(PP[:, 3, sl_], br, Wim[:, sl_], ALU.mult), r=["Wim"], x=[brn], w=["PP"])
            V(tt(PP[:, 2, sl_], bi, Wre[:, sl_], ALU.mult), r=["Wre"], x=[bin_], w=["PP"])
            V(tt(PP[:, 1, sl_], bi, Wim[:, sl_], ALU.mult), r=["Wim"], x=[bin_], w=["PP"])
        lre, lim = Ere[:, :, 1], Eim[:, :, 1]
        V(tt(sm8[:, 0], lre, Xst[:, 0], ALU.mult), r=["E", "Xst"], w=["sm8"])
        V(tt(sm8[:, 1], lim, Xst[:, 1], ALU.mult), r=["E", "Xst"], w=["sm8"])
        V(tt(cst[:, 0], sm8[:, 0], sm8[:, 1], ALU.subtract), r=["sm8"], w=["cst"])
        V(tt(sm8[:, 2], lre, Xst[:, 1], ALU.mult), r=["E", "Xst"], w=["sm8"])
        V(tt(sm8[:, 3], lim, Xst[:, 0], ALU.mult), r=["E", "Xst"], w=["sm8"])
        V(tt(cst[:, 1], sm8[:, 2], sm8[:, 3], ALU.add), r=["sm8"], w=["cst"])
        zcv = zc.rearrange("p c (k t) -> p c k t", t=128)
        for hh in range(2):
            bzr, bzrn = nb("E")
            bzi, bzin = nb("E")
            for kk in range(4):
                k = hh * 4 + kk
                ks = slice(k * 128, (k + 1) * 128)
                o_ = slice(kk * 128, (kk + 1) * 128)
                P(mm(bzr[:, o_], PP[:, 0, ks], tri, True, False), r=["PP", "tri"], x=[bzrn])
                P(mm(bzr[:, o_], PP[:, 1, ks], ntri, False, True), r=["PP", "ntri"], x=[bzrn])
                P(mm(bzi[:, o_], PP[:, 2, ks], tri, True, False), r=["PP", "tri"], x=[bzin])
                P(mm(bzi[:, o_], PP[:, 3, ks], tri, False, True), r=["PP", "tri"], x=[bzin])
            V(tt(zcv[:, 0, hh * 4:(hh + 1) * 4, :], bzr.rearrange("p (k t) -> p k t", t=128), cst[:, 0, hh * 4:(hh + 1) * 4].unsqueeze(2).to_broadcast([128, 4, 128]), ALU.add), r=["cst"], x=[bzrn], w=["zc"])
            V(tt(zcv[:, 1, hh * 4:(hh + 1) * 4, :], bzi.rearrange("p (k t) -> p k t", t=128), cst[:, 1, hh * 4:(hh + 1) * 4].unsqueeze(2).to_broadcast([128, 4, 128]), ALU.add), r=["cst"], x=[bzin], w=["zc"])
        Ere2, Eim2 = Ere.rearrange("p k t -> p (k t)"), Eim.rearrange("p k t -> p (k t)")
        V(tt(QQ[:, 0], Ere2, zc[:, 0], ALU.mult), r=["E", "zc"], w=["PP"])
        V(tt(QQ[:, 1], Eim2, zc[:, 1], ALU.mult), r=["E", "zc"], w=["PP"])
        V(tt(QQ[:, 2], Ere2, zc[:, 1], ALU.mult), r=["E", "zc"], w=["PP"])
        V(tt(QQ[:, 3], Eim2, zc[:, 0], ALU.mult), r=["E", "zc"], w=["PP"])
        e7r, e7i = Ere[:, :, 127], Eim[:, :, 127]
        z7r, z7i = zcv[:, 0, :, 127], zcv[:, 1, :, 127]
        V(tt(sm8[:, 4], e7r, z7r, ALU.mult), r=["E", "zc"], w=["sm8b"])
        V(tt(sm8[:, 5], e7i, z7i, ALU.mult), r=["E", "zc"], w=["sm8b"])
        V(tt(sm8[:, 6], e7r, z7i, ALU.mult), r=["E", "zc"], w=["sm8b"])
        V(tt(sm8[:, 7], e7i, z7r, ALU.mult), r=["E", "zc"], w=["sm8b"])
        V(tt(Xst[:, 0], sm8[:, 4], sm8[:, 5], ALU.subtract), r=["sm8b"], w=["Xst"])
        V(tt(Xst[:, 1], sm8[:, 6], sm8[:, 7], ALU.add), r=["sm8b"], w=["Xst"])
        by, byn = nb("E")
        QQv = QQ.rearrange("p q (k t) -> p q k t", t=128)
        for k in range(8):
            o_ = by[:, k * 32:(k + 1) * 32]
            P(mm(o_, QQv[:, 0, k, :], Cblk[:, 0, k, :], True, False), r=["PP", "Cblk"], x=[byn])
            P(mm(o_, QQv[:, 1, k, :], nCblk[:, 0, k, :], False, False), r=["PP", "nCblk"], x=[byn])
            P(mm(o_, QQv[:, 2, k, :], nCblk[:, 1, k, :], False, False), r=["PP", "nCblk"], x=[byn])
            P(mm(o_, QQv[:, 3, k, :], nCblk[:, 1, k, :], False, False), r=["PP", "nCblk"], x=[byn])
            P(mm(o_, uT[:, k // 4, :], Dblk[:, k // 4, (k % 4) * 32:(k % 4) * 32 + 32], False, True), r=["uT", "Dblk"], x=[byn])
        A(act(ysb, by[:, 0:256], AF.Copy), x=[byn], w=["ysb"])
        V(tt(g1, ysb, ysb, ALU.mult), r=["ysb"], w=["g1"])
        V(ts(g1, g1, 0.044715, ALU.mult, 1.0, ALU.add), r=["g1"], w=["g1"])
        V(tt(g1, g1, ysb, ALU.mult), r=["g1", "ysb"], w=["g1"])
        A(act(g2, g1, AF.Tanh, scale=math.sqrt(2.0 / math.pi)), r=["g1"], w=["g2"])
        V(ts(g2, g2, 0.5, ALU.mult, 0.5, ALU.add), r=["g2"], w=["g2"])
        V(tt(zf, g2, ysb, ALU.mult), r=["g2", "ysb"], w=["zf"])
        V(lambda e: e.tensor_copy(out=zb, in_=zf), r=["zf"], w=["zb"])
        bk, bn = nb("E")
        bkb = bk.bitcast(BF16)
        for j in range(2):
            P(tp(bkb[:, j * 128:(j + 1) * 128], zb[:, j * 128:(j + 1) * 128]), r=["zb", "ident"], x=[bn])
        A(act(zT.rearrange("p k t -> p (k t)"), bkb[:, 0:256], AF.Copy), x=[bn], w=["zT"])
        bg, bgn = nb("E")
        for j in range(2):
            P(mm(bg[:, 0:256], zT[:, j, :], w_glu[:, j, :], j == 0, j == 1), r=["zT", "w_glu"], x=[bgn])
        V(tt(g1, bg[:, 0:256], bglu, ALU.add), r=["bglu"], x=[bgn], w=["g1"])
        A(act(g2, g1, AF.Tanh, scale=0.5), r=["g1"], w=["g2"])
        V(ts(g2, g2, 0.5, ALU.mult, 0.5, ALU.add), r=["g2"], w=["g2"])
        V(tt(yssm, g2, zf, ALU.mult), r=["g2", "zf"], w=["yssm" + sfx])
        A(act(sqj[:, 0:256], yssm, AF.Square, accum=stat[:, 10:11]), r=["yssm" + sfx], w=["stat10" + sfx])

        whs = [1] if n == 0 else [0, 1]
        for g in range(2):
            bo, bon = nb("E")
            ps_ = []
            for wi, which in enumerate(whs):
                ksl = (1 - sl) if which == 0 else sl
                bs, bsn = nb("E")
                P(mm(bs, kTs[ksl][64 * g:64 * g + 64, :], qT[64 * g:64 * g + 64, :, :].rearrange("p k t -> p (k t)"), True, True), r=["kT%d" % ksl, "qT"], x=[bsn])
                p_ = pS[g * 2 + which]
                pn = "pS%d" % (g * 2 + which)
                A(act(p_, bs, AF.Exp, scale=0.125), x=[bsn], w=[pn])
                V(tt(p_, p_, eswa[:, which, g * 512:(g + 1) * 512], ALU.mult), r=["eswa"], w=[pn])
                ps_.append((p_, pn, ksl, wi))
            bov = bo[:, 0:260].rearrange("p (h d) -> p h d", d=65)
            for hh in range(4):
                for (p_, pn, ksl, wi) in ps_:
                    P(mm(bov[:, hh, :], p_[:, hh * 128:(hh + 1) * 128], vaugs[ksl][:, g, :], wi == 0, wi == len(ps_) - 1), r=[pn, "vaug%d" % ksl], x=[bon])
            V(tt(den[:, 0:4], bov[:, :, 64], esink[:, g * 4:(g + 1) * 4], ALU.add), r=["esink"], x=[bon], w=["den" + sfx])
            V(lambda e: e.reciprocal(out=den[:, 0:4], in_=den[:, 0:4]), r=["den" + sfx], w=["den" + sfx])
            V(tt(oswa[:, g * 4:(g + 1) * 4, :], bov[:, :, 0:64], den[:, 0:4].unsqueeze(2).to_broadcast([128, 4, 64]), ALU.mult), r=["den" + sfx], x=[bon], w=["oswa" + sfx])
        A(act(sqj[:, 0:512], oswa.rearrange("p h d -> p (h d)"), AF.Square, accum=stat[:, 11:12]), r=["oswa" + sfx], w=["stat10" + sfx])

        rstd_pool(stat[:, 10:13], 3, [1.0 / 256, 1.0 / 512, 1.0 / 256], stat[:, 13:16], "stat10" + sfx, "stat13" + sfx)
        V(ts(mixed[:, 0:256], yssm, stat[:, 13:14], ALU.mult), r=["yssm" + sfx, "stat13" + sfx], w=["mixed" + sfx])
        V(ts(mixed[:, 256:768], oswa.rearrange("p h d -> p (h d)"), stat[:, 14:15], ALU.mult), r=["oswa" + sfx, "stat13" + sfx], w=["mixed" + sfx])
        V(ts(mixed[:, 768:1024], omla.rearrange("p h d -> p (h d)"), stat[:, 15:16], ALU.mult), r=["omla" + sfx, "stat13" + sfx], w=["mixed" + sfx])
        if dbg and l == 0:
            t.dma("pool", "dbg", lambda e: e.dma_start(out=dbg_d[n], in_=mixed), r=["mixed" + sfx])
        bk, bn = nb("L")
        bkb = bk.bitcast(BF16)
        for k in range(8):
            P(tp(bkb[:, k * 128:(k + 1) * 128], mixed[:, k * 128:(k + 1) * 128]), r=["mixed" + sfx, "ident"], x=[bn])
        A(act(mixT.rearrange("p k t -> p (k t)"), bkb, AF.Copy), x=[bn], w=["mixT" + sfx])
        o0, o0n = nb("L")
        o1, o1n = nb("L")
        for half, (ob, obn) in enumerate(((o0, o0n), (o1, o1n))):
            for k in range(8):
                P(mm(ob, mixT[:, k, :], w_out[:, k, half * 512:(half + 1) * 512], k == 0, k == 7), r=["mixT" + sfx, "w_out"], x=[obn])
        postnorm_store(l, n, o0, o0n, o1, o1n, h, hn, hbuf[n * 128:(n + 1) * 128, :])

    def phaseF(l):
        t.dma("sp", "gcols", lambda e: e.dma_start(out=gcols, in_=gcols_d[l]), w=["gcols"])
        t.dma("sp", "gpost", lambda e: e.dma_start(out=gpost, in_=gpost_d[l, 1]), w=["gpost"])
        cast_engs = ["act", "dve", "act", "dve", "pool"]
        ci = 0
        for k in range(8):
            for half in range(2):
                sg = stg[ci % 3]
                sgn = "stg%d" % (ci % 3)
                t.dma("sp", sgn, lambda e, k=k, half=half, sg=sg: e.dma_start(out=sg, in_=w_up_d[l, k * 128:(k + 1) * 128, half * 2048:(half + 1) * 2048]), w=[sgn], nbytes=1 << 20)
                dst = w_up[:, k, half * 2048:(half + 1) * 2048]
                eng = cast_engs[ci % 5]
                gc = gcols[:, 16 + k:17 + k]
                if eng == "act":
                    A(hint(lambda e, dst=dst, sg=sg, gc=gc: e.activation(out=dst, in_=sg, func=AF.Copy, scale=gc), dst), r=[sgn, "gcols"], w=["w_up_%d_%d" % (k, half)])
                else:
                    t.op(eng, ts(dst, sg, gc, ALU.mult), r=[sgn, "gcols"], w=["w_up_%d_%d" % (k, half)])
                ci += 1
        for i in range(16):
            sg = stg[ci % 3]
            sgn = "stg%d" % (ci % 3)
            t.dma("sp", sgn, lambda e, i=i, sg=sg: e.dma_start(out=sg.rearrange("p (k c) -> p k c", k=2), in_=w_dn_d[l, i * 256:(i + 1) * 256, :].rearrange("(k p) c -> p k c", p=128)), w=[sgn], nbytes=1 << 20)
            dst = w_dn[:, 2 * i:2 * i + 2, :].rearrange("p k c -> p (k c)")
            eng = cast_engs[ci % 5]
            if eng == "act":
                A(act(dst, sg, AF.Copy), r=[sgn], w=["w_dn_%d" % i])
            else:
                t.op(eng, hint(lambda e, dst=dst, sg=sg: e.tensor_copy(out=dst, in_=sg), dst), r=[sgn], w=["w_dn_%d" % i])
            ci += 1
        last = (l == DEPTH - 1)
        for n in range(NB):
            if last and n == 0:
                continue
            h, hn = load_h(l, n, False)
            prenorm(h, hn, n, "A")
            for mg in range(8):
                bk, bn = nb()
                for mi in range(4):
                    m = mg * 4 + mi
                    for k in range(8):
                        P(mm(bk[:, mi * 128:(mi + 1) * 128], w_up[:, k, m * 128:(m + 1) * 128], aT[:, k, :], k == 0, k == 7), r=["aT"] + ["w_up_%d_%d" % (k, mg // 4)], x=[bn])
                r_ = rl[mg % 2]
                rn = "rl%d" % (mg % 2)
                A(act(r_, bk, AF.Relu), x=[bn], w=[rn])
                (V if mg % 2 == 0 else G)(tt(hT[:, mg * 4:(mg + 1) * 4, :].rearrange("p m t -> p (m t)"), r_, r_, ALU.mult), r=[rn], w=["hT%d" % mg])
            o0, o0n = nb()
            o1, o1n = nb()
            for half, (ob, obn) in enumerate(((o0, o0n), (o1, o1n))):
                for m in range(32):
                    P(mm(ob, hT[:, m, :], w_dn[:, m, half * 512:(half + 1) * 512], m == 0, m == 31), r=["hT%d" % (m // 4), "w_dn_%d" % (m // 2)], x=[obn])
            dst = out_d[(n - 1) * 128:n * 128, :] if last else hbuf[n * 128:(n + 1) * 128, :]
            postnorm_store(l, n, o0, o0n, o1, o1n, h, hn, dst)

    for l in range(DEPTH):
        phaseM(l)
        t.barrier()
        phaseF(l)
        t.barrier()
    t.final_wait("sp", ["hnew0"])
    if dbg:
        t.final_wait("pool", ["mixed_0", "mixed_1"])
    t.emit()
    return nc


def _consts(NB):
    ident = np.eye(128, dtype=np.float32)
    s = np.arange(128)
    tri = (s[:, None] <= s[None, :]).astype(np.float32)
    ones = np.ones((128, 128), np.float32)
    slopes = 2.0 ** (-8.0 * np.arange(1, 9, dtype=np.float32) / 8)
    key, q = s[:, None], s[None, :]
    e = np.zeros((128, 2, 8, 128), np.float32)
    for h in range(8):
        dprev = (q + 128 - key).astype(np.float32)
        e[:, 0, h, :] = np.where(key > q, np.exp(-slopes[h] * dprev), 0.0)
        dcur = (q - key).astype(np.float32)
        e[:, 1, h, :] = np.where(key <= q, np.exp(-slopes[h] * dcur), 0.0)
    mask4 = np.tile(tri[:, None, :], (1, 4, 1)).reshape(128, 512)
    cbf = np.concatenate([ident, tri, ones, e.reshape(128, 2048), mask4], axis=1).astype(np.float32)
    half = 16
    inv_freq = (10000.0 ** (-np.arange(half, dtype=np.float32) / half)).astype(np.float32)
    pos = (np.arange(NB * 128) - PAD).astype(np.float32)
    ang = (pos[:, None] * inv_freq[None, :]).astype(np.float32)
    c = np.cos(ang).astype(np.float32).reshape(NB, 128, 16).transpose(1, 0, 2).reshape(128, NB * 16)
    sn = np.sin(ang).astype(np.float32).reshape(NB, 128, 16).transpose(1, 0, 2).reshape(128, NB * 16)
    tv = np.tile(np.arange(128, dtype=np.float32)[None, :], (128, 1))
    misc = np.zeros((128, 8), np.float32)
    misc[:, 0] = np.arange(128)
    misc[:, 1] = -np.arange(128)
    misc[:, 2] = (np.arange(128) >= PAD).astype(np.float32)
    misc[:, 3] = -0.5
    cf32 = np.concatenate([c, sn, tv, misc], axis=1).astype(np.float32)
    return cbf, cf32


def _layout(inp, NB, DEPTH):
    f = lambda a: np.ascontiguousarray(np.asarray(a, dtype=np.float32))
    w_in = f(inp["w_in"])[:DEPTH]
    u, q, k, v, cq, ckv, kr = np.split(w_in, [256, 768, 896, 1024, 1216, 1344], axis=2)
    qh = q.reshape(DEPTH, 1024, 8, 64)
    qperm = np.concatenate([np.concatenate([qh[:, :, j], qh[:, :, j + 4]], axis=2) for j in range(4)], axis=2)
    d = {}
    d["w_inF"] = f(np.concatenate([qperm, u, k], axis=2))
    d["w_inT"] = f(np.concatenate([v, cq, ckv, kr], axis=2))
    d["w_out"] = f(inp["w_out"])[:DEPTH]
    d["w_up"] = f(inp["w_mlp_up"])[:DEPTH]
    d["w_dn"] = f(inp["w_mlp_down"])[:DEPTH]
    wuq = np.zeros((DEPTH, 256, 384), np.float32)
    wuq[:, :192] = f(inp["mla_w_uq"])[:DEPTH]
    d["w_uq"] = wuq
    d["w_uk"] = f(inp["mla_w_uk"])[:DEPTH]
    d["w_uv"] = f(inp["mla_w_uv"])[:DEPTH]
    d["w_glu"] = f(inp["ssm_w_glu"])[:DEPTH]
    col = lambda g: f(g)[:DEPTH].reshape(DEPTH, -1, 128).transpose(0, 2, 1)
    qn = np.zeros((DEPTH, 256), np.float32)
    qn[:, :192] = f(inp["mla_q_norm"])[:DEPTH]
    d["gcols"] = f(np.concatenate([col(inp["norm_pre_mix"]), col(inp["norm_heads"]), col(inp["norm_pre_mlp"]),
                                   qn.reshape(DEPTH, 2, 128).transpose(0, 2, 1), col(inp["mla_kv_norm"])], axis=2))
    bc = lambda a: np.broadcast_to(f(a)[:DEPTH][:, None, :], (DEPTH, 128, f(a).shape[-1]))
    d["gpost"] = f(np.stack([bc(inp["norm_post_mix"]), bc(inp["norm_post_mlp"])], axis=1))
    d["bglu"] = f(bc(inp["ssm_b_glu"]))
    d["sinks"] = f(bc(inp["swa_sinks"]))
    Bb = np.zeros((DEPTH, 128, 2, 2, 512), np.float32)
    for ri, nm in enumerate(("ssm_b_re", "ssm_b_im")):
        b = f(inp[nm])[:DEPTH]
        for j in range(2):
            for gl in range(8):
                Bb[:, gl * 16:(gl + 1) * 16, ri, j, gl * 64:(gl + 1) * 64] = b[:, 8 * j + gl].transpose(0, 2, 1)
    d["Bblk"] = f(Bb.reshape(DEPTH, 128, 2048))
    Cb = np.zeros((DEPTH, 128, 2, 8, 32), np.float32)
    for ri, nm in enumerate(("ssm_c_re", "ssm_c_im")):
        c = f(inp[nm])[:DEPTH]
        for k_ in range(8):
            for gi in range(2):
                Cb[:, gi * 64:(gi + 1) * 64, ri, k_, gi * 16:(gi + 1) * 16] = c[:, 2 * k_ + gi].transpose(0, 2, 1)
    d["Cblk"] = f(Cb.reshape(DEPTH, 128, 512))
    Db = np.zeros((DEPTH, 128, 2, 128), np.float32)
    dd = f(inp["ssm_d"])[:DEPTH]
    for j in range(2):
        for c_ in range(128):
            Db[:, c_, j, c_] = dd[:, 128 * j + c_]
    d["Dblk"] = f(Db.reshape(DEPTH, 128, 256))
    are = f(inp["ssm_a_re"])[:DEPTH].reshape(DEPTH, 1024)
    aim = f(inp["ssm_a_im"])[:DEPTH].reshape(DEPTH, 1024)
    ls = np.repeat(f(inp["ssm_log_step"])[:DEPTH], 64, axis=1)
    d["abc"] = f(np.stack([np.broadcast_to(a[:, None, :], (DEPTH, 128, 1024)) for a in (are, aim, ls)], axis=1))
    colS = lambda a: a.reshape(DEPTH, 8, 128).transpose(0, 2, 1)
    d["acol"] = f(np.concatenate([colS(are), colS(aim), colS(ls)], axis=2))
    cbf, cf32 = _consts(NB)
    d["cbf"], d["cf32"] = cbf, cf32
    return d


_CACHE = {}
DBG = False
DBG_OUT = {}


def kernel(**inputs):
    x = np.asarray(inputs["x"], dtype=np.float32)
    B, S, D = x.shape
    NB = S // 128 + 1
    DEPTH = np.asarray(inputs["w_in"]).shape[0]
    shared = _layout(inputs, NB, DEPTH)
    meta = np.asarray(inputs["meta_tokens"], dtype=np.float32)
    key = (NB, DEPTH)
    if key not in _CACHE:
        _CACHE[key] = build(NB, DEPTH, DBG)
    nc = _CACHE[key]
    in_maps = []
    for b in range(B):
        m = dict(shared)
        m["xin"] = np.ascontiguousarray(np.concatenate([np.zeros((PAD, D), np.float32), meta, x[b]], axis=0))
        in_maps.append(m)
    res = run_bass_kernel_spmd(nc, in_maps, core_ids=list(range(B)))
    if DBG:
        DBG_OUT["dbg"] = np.stack([np.asarray(r["dbg"]) for r in res.results], axis=0)
    return np.stack([np.asarray(r["out"], dtype=np.float32) for r in res.results], axis=0)
```

```python
import math
import numpy as np
import concourse.bass as bass
import concourse.mybir as mybir
from concourse.bass_utils import run_bass_kernel_spmd

F32 = mybir.dt.float32
BF16 = mybir.dt.bfloat16
I32 = mybir.dt.int32
ALU = mybir.AluOpType
AF = mybir.ActivationFunctionType
AX = mybir.AxisListType

ENGS = ("pe", "act", "dve", "pool", "sp")
EPS = 1e-6
PAD = 112


def _cost(eng, n, kind):
    if kind == "dma":
        return 150.0
    if eng == "pe":
        return max(n / 2.05, 70.0) + 5.0
    if eng == "act":
        return 220.0 + 0.8 * n
    if eng == "dve":
        return 100.0 + 0.9 * n
    if eng == "pool":
        return 170.0 + 1.1 * n
    return 100.0


class Tr:
    WINDOW = 5000
    KCAND = 32

    def __init__(self, nc):
        self.nc = nc
        self.ops = []
        self.res = {}
        self.seg = 0
        self.dma_cnt = {}

    def _add(self, eng, kind, fn, r, w, x, key=None, n=None):
        oid = len(self.ops)
        deps = {}
        for name in r:
            st = self.res.setdefault(name, {"w": None, "r": []})
            if st["w"] is not None:
                deps[st["w"]] = True
        for name in list(w) + list(x):
            st = self.res.setdefault(name, {"w": None, "r": []})
            if st["w"] is not None:
                deps.setdefault(st["w"], False)
            for rid in st["r"]:
                deps.setdefault(rid, False)
        if n is None:
            n = getattr(fn, "n", 256) if fn is not None else 0
        self.ops.append({"id": oid, "eng": eng, "kind": kind, "fn": fn, "n": n, "deps": list(deps.items()),
                         "key": key, "seg": self.seg})
        for name in r:
            self.res[name]["r"].append(oid)
        for name in list(w) + list(x):
            st = self.res[name]
            st["w"] = oid
            st["r"] = []
        return oid

    def op(self, eng, fn, r=(), w=(), x=(), n=None):
        self._add(eng, "op", fn, r, w, x, n=n)

    def dma(self, q, key, fn, r=(), w=(), nbytes=65536):
        self._add(q, "dma", fn, r, w, (), key="dma:" + key, n=nbytes)

    def barrier(self):
        self.seg += 1

    def final_wait(self, q, names):
        self._add(q, "wait", None, (), names, ())

    def _schedule(self):
        import heapq
        ops = self.ops
        nseg = self.seg + 1
        order = {e: [] for e in ENGS}
        seg_ops = [[] for _ in range(nseg)]
        for o in ops:
            seg_ops[o["seg"]].append(o["id"])
        finish = {}
        tbase = 0.0
        for sg in range(nseg):
            ids = seg_ops[sg]
            if not ids:
                continue
            idset_lo, idset_hi = ids[0], ids[-1]
            ndeps = {}
            users = {}
            for i in ids:
                c = 0
                for d, _ in ops[i]["deps"]:
                    if d >= idset_lo:
                        c += 1
                        users.setdefault(d, []).append(i)
                ndeps[i] = c
            ready = {e: [] for e in ENGS}
            for i in ids:
                if ndeps[i] == 0:
                    heapq.heappush(ready[ops[i]["eng"]], i)
            free_at = {e: tbase for e in ENGS}
            nsched = 0
            lo_ptr = 0
            scheduled = set()
            total = len(ids)
            while nsched < total:
                while lo_ptr < total and ids[lo_ptr] in scheduled:
                    lo_ptr += 1
                lim = ids[lo_ptr] + self.WINDOW if lo_ptr < total else 1 << 60
                best = None
                for e in ENGS:
                    h = ready[e]
                    if not h:
                        continue
                    cands = heapq.nsmallest(self.KCAND, h)
                    for i in cands:
                        if i > lim and best is not None:
                            break
                        st = free_at[e]
                        for d, _ in ops[i]["deps"]:
                            if d >= idset_lo:
                                lat = 60.0 if ops[d]["eng"] == e and ops[d]["kind"] == "op" else 150.0
                                fd = finish[d] + lat
                                if fd > st:
                                    st = fd
                        pri = (st + (0.0 if i <= lim else 1e9), i)
                        if best is None or pri < best[0]:
                            best = (pri, e, i, st)
                _, e, i, st = best
                ready[e].remove(i)
                heapq.heapify(ready[e])
                o = ops[i]
                c = _cost(e, o["n"], o["kind"])
                if o["kind"] == "dma":
                    free_at[e] = st + (1200.0 if e == "pool" else 150.0)
                    finish[i] = st + 2500.0 + o["n"] / 150.0
                elif o["kind"] == "wait":
                    free_at[e] = st
                    finish[i] = st
                else:
                    free_at[e] = st + c
                    finish[i] = st + c
                order[e].append(i)
                scheduled.add(i)
                nsched += 1
                for u in users.get(i, ()):
                    ndeps[u] -= 1
                    if ndeps[u] == 0:
                        heapq.heappush(ready[ops[u]["eng"]], u)
            tbase = max(max(free_at.values()), max(finish[i] for i in ids))
        return order

    def emit(self):
        nc = self.nc
        ops = self.ops
        order = self._schedule()
        pos = {}
        for e in ENGS:
            for p, i in enumerate(order[e]):
                pos[i] = p
        dma_idx = {}
        kq = {}
        cnt = {}
        for e in ENGS:
            for i in order[e]:
                o = ops[i]
                if o["kind"] == "dma":
                    assert kq.setdefault(o["key"], e) == e, o["key"]
                    k = cnt.get(o["key"], 0)
                    cnt[o["key"]] = k + 1
                    dma_idx[i] = k
        nseg = self.seg + 1
        last_in_seg = [dict() for _ in range(nseg)]
        for e in ENGS:
            for i in order[e]:
                o = ops[i]
                sg = o["seg"]
                if o["kind"] == "op":
                    last_in_seg[sg][e] = i
                elif o["kind"] == "dma":
                    last_in_seg[sg][o["key"]] = i
        waits = {}
        sig = set()
        for e in ENGS:
            wm = {}
            cur_seg = -1
            for i in order[e]:
                o = ops[i]
                need = []
                if o["seg"] != cur_seg:
                    for sg in range(max(cur_seg, 0), o["seg"]):
                        for src, li in last_in_seg[sg].items():
                            if src != e:
                                need.append(li)
                    cur_seg = o["seg"]
                for d, raw in o["deps"]:
                    od = ops[d]
                    if od["kind"] == "wait":
                        continue
                    if od["kind"] == "op" and od["eng"] == e and o["kind"] == "op":
                        if e == "pe" or not raw:
                            continue
                    need.append(d)
                wl = []
                for d in need:
                    od = ops[d]
                    src = od["key"] if od["kind"] == "dma" else od["eng"]
                    p = dma_idx[d] if od["kind"] == "dma" else pos[d]
                    if wm.get(src, -1) >= p:
                        continue
                    wm[src] = p
                    wl.append(d)
                    if od["kind"] == "op":
                        sig.add(d)
                waits[i] = wl
        sems = {e: nc.alloc_semaphore("S_" + e) for e in ENGS}
        for k in cnt:
            sems[k] = nc.alloc_semaphore("D_" + k[4:])
        val = {}
        for e in ENGS:
            c = 0
            for i in order[e]:
                if ops[i]["kind"] == "op" and i in sig:
                    c += 1
                    val[i] = c

        def run(ename, eng):
            for i in order[ename]:
                o = ops[i]
                for d in waits[i]:
                    od = ops[d]
                    if od["kind"] == "dma":
                        eng.wait_ge(sems[od["key"]], 16 * (dma_idx[d] + 1))
                    else:
                        eng.wait_ge(sems[od["eng"]], val[d])
                if o["kind"] == "op":
                    ins = o["fn"](eng)
                    if i in sig:
                        ins.then_inc(sems[ename], 1)
                elif o["kind"] == "dma":
                    o["fn"](eng).then_inc(sems[o["key"]], 16)

        with nc.Block() as block:
            @block.tensor
            def _(e):
                run("pe", e)

            @block.scalar
            def _(e):
                run("act", e)

            @block.vector
            def _(e):
                run("dve", e)

            @block.gpsimd
            def _(e):
                run("pool", e)

            @block.sync
            def _(e):
                run("sp", e)


def build(NB, DEPTH, dbg=None):
    L = NB * 128
    S = L - 128
    nc = bass.Bass("TRN2", target_bir_lowering=False)
    t = Tr(nc)

    def din(name, shape):
        return nc.dram_tensor(name, list(shape), F32, kind="ExternalInput")

    xin = din("xin", [L, 1024])
    out_d = nc.dram_tensor("out", [S, 1024], F32, kind="ExternalOutput")
    hbuf = nc.dram_tensor("hbuf", [L, 1024], F32, kind="Internal")
    dbg_d = nc.dram_tensor("dbg", [NB, 128, 1024], F32, kind="ExternalOutput") if dbg else None
    w_inF_d = din("w_inF", [DEPTH, 1024, 896])
    w_inT_d = din("w_inT", [DEPTH, 1024, 480])
    w_out_d = din("w_out", [DEPTH, 1024, 1024])
    w_up_d = din("w_up", [DEPTH, 1024, 4096])
    w_dn_d = din("w_dn", [DEPTH, 4096, 1024])
    w_uq_d = din("w_uq", [DEPTH, 256, 384])
    w_uk_d = din("w_uk", [DEPTH, 128, 256])
    w_uv_d = din("w_uv", [DEPTH, 128, 256])
    w_glu_d = din("w_glu", [DEPTH, 256, 256])
    gcols_d = din("gcols", [DEPTH, 128, 27])
    gpost_d = din("gpost", [DEPTH, 2, 128, 1024])
    bglu_d = din("bglu", [DEPTH, 128, 256])
    sinks_d = din("sinks", [DEPTH, 128, 8])
    Bblk_d = din("Bblk", [DEPTH, 128, 2048])
    Cblk_d = din("Cblk", [DEPTH, 128, 512])
    Dblk_d = din("Dblk", [DEPTH, 128, 256])
    abc_d = din("abc", [DEPTH, 3, 128, 1024])
    acol_d = din("acol", [DEPTH, 128, 24])
    cbf_d = din("cbf", [128, 128 * 3 + 2048 + 512])
    cf32_d = din("cf32", [128, NB * 32 + 128 + 8])

    arena = nc.alloc_sbuf_tensor("arena", [128, 104000], BF16)[:, :]
    state = {"off": 0}

    def carve(shape, dtype, region=None):
        n = int(np.prod(shape[1:]))
        esz = 4 if dtype in (F32, I32) else 2
        nb2 = (n * esz + 63) // 64 * 32
        reg = region if region is not None else state
        off = reg["off"]
        reg["off"] = off + nb2
        assert reg["off"] <= reg.get("lim", 104000), (shape, reg)
        ap = arena[0:shape[0], off:off + n * esz // 2]
        if esz == 4:
            ap = ap.bitcast(dtype)
        if len(shape) == 3:
            ap = ap.rearrange("p (a b) -> p a b", b=shape[2])
        elif len(shape) == 4:
            ap = ap.rearrange("p (a b c) -> p a b c", b=shape[2], c=shape[3])
        return ap

    ident = carve([128, 128], BF16)
    tri = carve([128, 128], BF16)
    ones = carve([128, 128], BF16)
    eswa = carve([128, 2, 1024], BF16)
    mask4 = carve([128, 512], BF16)
    ntri = carve([128, 128], BF16)
    ropec = carve([128, NB, 16], F32)
    ropes = carve([128, NB, 16], F32)
    tvals = carve([128, 128], F32)
    misc = carve([128, 8], F32)
    gcols = carve([128, 27], F32)
    rtm = carve([128, 4, 2, 4], F32)
    rst = {"i": 0}
    gpost = carve([128, 1024], F32)
    hs = [carve([128, 1024], F32) for _ in range(2)]
    a_bf = carve([128, 1024], BF16)
    aT = carve([128, 8, 128], BF16)
    sqj = carve([128, 1024], BF16)
    stats = [carve([128, 16], F32) for _ in range(2)]
    hnew = [carve([128, 1024], F32)] * 2
    union0 = state["off"]

    M = {"off": union0, "lim": 104000}
    w_inF = carve([128, 8, 896], BF16, M)
    w_inT = carve([128, 8, 480], BF16, M)
    w_out = carve([128, 8, 1024], BF16, M)
    w_uq = carve([128, 2, 384], BF16, M)
    w_uk = carve([128, 256], BF16, M)
    w_uv = carve([128, 256], BF16, M)
    w_glu = carve([128, 2, 256], BF16, M)
    Bblk = carve([128, 4, 512], BF16, M)
    Cblk = carve([128, 2, 8, 32], BF16, M)
    nCblk = carve([128, 2, 8, 32], BF16, M)
    Dblk = carve([128, 2, 128], BF16, M)
    bglu = carve([128, 256], F32, M)
    esink = carve([128, 8], F32, M)
    Wre = carve([128, 1024], F32, M)
    Wim = carve([128, 1024], F32, M)
    Ere = carve([128, 8, 128], F32, M)
    Eim = carve([128, 8, 128], F32, M)
    acol = carve([128, 24], F32, M)
    Xst = carve([128, 2, 8], F32, M)
    cst = carve([128, 2, 8], F32, M)
    sm8 = carve([128, 8, 8], F32, M)
    PP = carve([128, 4, 1024], BF16, M)
    QQ = PP
    zc = carve([128, 2, 1024], F32, M)
    Lk = max(L, 2816)
    KT = carve([128, 4, Lk], BF16, M)
    Tal = {"off": M["off"] - (4 * Lk * 2 + 63) // 64 * 32, "lim": M["off"]}
    T1 = carve([128, 1024], F32, Tal)
    T2 = carve([128, 1024], F32, Tal)
    T3 = carve([128, 1024], F32, Tal)
    T4 = carve([128, 1024], F32, Tal)
    TI = carve([128, 1024], I32, Tal)
    uT = carve([128, 2, 128], BF16, M)
    qT = carve([128, 4, 128], BF16, M)
    kTs = [carve([128, 128], BF16, M) for _ in range(2)]
    vaugs = [carve([128, 2, 65], BF16, M) for _ in range(2)]
    cqkvs = [carve([128, 320], BF16, M) for _ in range(2)]
    krfs = [carve([128, 32], F32, M) for _ in range(2)]
    krb = carve([128, 96], BF16, M)
    dq = carve([128, 128], BF16, M)
    dkv = carve([128, 128], BF16, M)
    cqT = carve([128, 2, 128], BF16, M)
    cT = carve([128, 128], BF16, M)
    qf = carve([128, 4, 96], F32, M)
    qb = carve([128, 4, 96], BF16, M)
    rtmp = carve([128, 4, 4, 16], F32, M)
    QTs = [carve([128, 4, 128], BF16, M) for _ in range(2)]
    Vaug = carve([128, NB, 4, 65], BF16, M)
    pTs = [carve([128, 512], BF16, M) for _ in range(3)]
    pS = [carve([128, 512], BF16, M) for _ in range(4)]
    ysb = carve([128, 256], F32, M)
    g1 = carve([128, 256], F32, M)
    g2 = carve([128, 256], F32, M)
    zf = carve([128, 256], F32, M)
    zb = carve([128, 256], BF16, M)
    zT = carve([128, 2, 128], BF16, M)
    yssms = [carve([128, 256], F32, M) for _ in range(2)]
    oswas = [carve([128, 8, 64], F32, M) for _ in range(2)]
    omlas = [carve([128, 4, 64], F32, M) for _ in range(2)]
    dens = [carve([128, 8], F32, M) for _ in range(2)]
    mixeds = [carve([128, 1024], BF16, M) for _ in range(2)]
    mixTs = [carve([128, 8, 128], BF16, M) for _ in range(2)]

    Fr = {"off": union0, "lim": 104000}
    w_up = carve([128, 8, 4096], BF16, Fr)
    w_dn = carve([128, 32, 1024], BF16, Fr)
    hT = carve([128, 32, 128], BF16, Fr)
    rl = [carve([128, 512], F32, Fr) for _ in range(2)]
    stg = [carve([128, 2048], F32, Fr) for _ in range(3)]

    print("SBUF carve: persistent", union0 * 2, "M end", M["off"] * 2, "F end", Fr["off"] * 2)
    banks = [nc.alloc_psum_tensor("pb%d" % i, [128, 512], F32)[:, :] for i in range(8)]
    bstate = {"A": 0, "E": 0, "L": 0}
    bgroups = {"A": list(range(8)), "E": [0, 1, 2, 3, 4], "L": [5, 6, 7]}

    def nb(grp="A"):
        g = bgroups[grp]
        i = g[bstate[grp] % len(g)]
        bstate[grp] += 1
        return banks[i], "pb%d" % i

    V = lambda fn, r=(), w=(), x=(): t.op("dve", fn, r, w, x)
    A = lambda fn, r=(), w=(), x=(): t.op("act", fn, r, w, x)
    P = lambda fn, r=(), w=(), x=(): t.op("pe", fn, r, w, x)
    G = lambda fn, r=(), w=(), x=(): t.op("pool", fn, r, w, x)

    def fsz(ap):
        return int(np.prod(ap.shape[1:]))

    def hint(f, ap):
        f.n = fsz(ap)
        return f

    def tt(out, a, b, op):
        return hint(lambda e: e.tensor_tensor(out=out, in0=a, in1=b, op=op), out)

    def ts(out, a, s1, op0, s2=None, op1=None):
        if op1 is None:
            return hint(lambda e: e.tensor_scalar(out=out, in0=a, scalar1=s1, scalar2=None, op0=op0), out)
        return hint(lambda e: e.tensor_scalar(out=out, in0=a, scalar1=s1, scalar2=s2, op0=op0, op1=op1), out)

    def act(out, in_, func, scale=1.0, accum=None):
        if accum is None:
            return hint(lambda e: e.activation(out=out, in_=in_, func=func, scale=scale), out)
        return hint(lambda e: e.activation(out=out, in_=in_, func=func, scale=scale, accum_out=accum), out)

    def mm(out, lhsT, rhs, start, stop):
        return hint(lambda e: e.matmul(out, lhsT=lhsT, rhs=rhs, start=start, stop=stop), out)

    def tp(out, in_):
        return hint(lambda e: e.transpose(out, in_, ident[0:in_.shape[0], 0:in_.shape[0]]), out)

    t.dma("pool", "c_ident", lambda e: e.dma_start(out=ident, in_=cbf_d[:, 0:128]), w=["ident"])
    t.dma("pool", "c_tri", lambda e: e.dma_start(out=tri, in_=cbf_d[:, 128:256]), w=["tri"])
    t.dma("pool", "c_ones", lambda e: e.dma_start(out=ones, in_=cbf_d[:, 256:384]), w=["ones"])
    t.dma("pool", "c_eswa", lambda e: e.dma_start(out=eswa.rearrange("p a b -> p (a b)"), in_=cbf_d[:, 384:384 + 2048]), w=["eswa"])
    t.dma("pool", "c_mask4", lambda e: e.dma_start(out=mask4, in_=cbf_d[:, 2432:2944]), w=["mask4"])
    t.dma("sp", "c_ropec", lambda e: e.dma_start(out=ropec.rearrange("p a b -> p (a b)"), in_=cf32_d[:, 0:NB * 16]), w=["ropec"])
    t.dma("sp", "c_ropes", lambda e: e.dma_start(out=ropes.rearrange("p a b -> p (a b)"), in_=cf32_d[:, NB * 16:NB * 32]), w=["ropes"])
    t.dma("sp", "c_tvals", lambda e: e.dma_start(out=tvals, in_=cf32_d[:, NB * 32:NB * 32 + 128]), w=["tvals"])
    t.dma("sp", "c_misc", lambda e: e.dma_start(out=misc, in_=cf32_d[:, NB * 32 + 128:NB * 32 + 136]), w=["misc"])
    V(ts(ntri, tri, -1.0, ALU.mult), r=["tri"], w=["ntri"])
    tcol, ntcol, valid0, nhalf = misc[:, 0:1], misc[:, 1:2], misc[:, 2:3], misc[:, 3:4]
    TWO_PI = 2.0 * math.pi

    def rstd_pool(ss_ap, ncol, invs, out_ap, rname, wname):
        for c, inv in enumerate(invs):
            G(ts(out_ap[:, c:c + 1], ss_ap[:, c:c + 1], inv, ALU.mult, EPS, ALU.add), r=[rname], w=[wname])
        G(tt(out_ap[:, 0:ncol], out_ap[:, 0:ncol], nhalf.to_broadcast([128, ncol]), ALU.pow), r=[wname, "misc"], w=[wname])

    def sincos(outs, outc, ang, r, wn):
        for dst, shift in ((outs, 0.0), (outc, 0.5 * math.pi)):
            V(ts(T3, ang, 1.0 / TWO_PI, ALU.mult, shift / TWO_PI, ALU.add), r=r, w=["T3"])
            V(lambda e: e.tensor_copy(out=TI, in_=T3), r=["T3"], w=["TI"])
            V(lambda e: e.tensor_copy(out=T4, in_=TI), r=["TI"], w=["T4"])
            V(tt(T3, T3, T4, ALU.subtract), r=["T3", "T4"], w=["T3"])
            V(ts(T3, T3, TWO_PI, ALU.mult, math.pi, ALU.min), r=["T3"], w=["T3"])
            V(ts(T3, T3, -math.pi, ALU.max), r=["T3"], w=["T3"])
            A(act(dst, T3, AF.Sin), r=["T3"], w=[wn])

    def phaseM(l):
        def wload(key, dst, src, g0=None, nk=1):
            t.dma("pool", key, lambda e: e.dma_start(out=dst, in_=src), w=[key])
            if g0 is not None:
                for k in range(nk):
                    d = dst[:, k] if nk > 1 or len(dst.shape) == 3 else dst
                    V(ts(d, d, gcols[:, g0 + k:g0 + k + 1], ALU.mult), r=["gcols"], w=[key])
        t.dma("sp", "gcols", lambda e: e.dma_start(out=gcols, in_=gcols_d[l]), w=["gcols"])
        t.dma("sp", "gpost", lambda e: e.dma_start(out=gpost, in_=gpost_d[l, 0]), w=["gpost"])
        t.dma("sp", "bglu", lambda e: e.dma_start(out=bglu, in_=bglu_d[l]), w=["bglu"])
        t.dma("sp", "esink", lambda e: e.dma_start(out=esink, in_=sinks_d[l]), w=["esink"])
        t.dma("sp", "acol", lambda e: e.dma_start(out=acol, in_=acol_d[l]), w=["acol"])
        A(act(esink, esink, AF.Exp), r=["esink"], w=["esink"])
        wload("w_inF", w_inF, w_inF_d[l].rearrange("(k p) c -> p k c", p=128), 0, 8)
        wload("w_inT", w_inT, w_inT_d[l].rearrange("(k p) c -> p k c", p=128), 0, 8)
        wload("w_out", w_out, w_out_d[l].rearrange("(k p) c -> p k c", p=128), 8, 8)
        wload("w_uq", w_uq, w_uq_d[l].rearrange("(k p) c -> p k c", p=128), 24, 2)
        wload("w_uk", w_uk, w_uk_d[l], None)
        V(ts(w_uk, w_uk, gcols[:, 26:27], ALU.mult), r=["gcols"], w=["w_uk"])
        wload("w_uv", w_uv, w_uv_d[l], None)
        V(ts(w_uv, w_uv, gcols[:, 26:27], ALU.mult), r=["gcols"], w=["w_uv"])
        wload("w_glu", w_glu, w_glu_d[l].rearrange("(k p) c -> p k c", p=128), None)
        wload("Bblk", Bblk.rearrange("p a b -> p (a b)"), Bblk_d[l], None)
        wload("Cblk", Cblk.rearrange("p a b c -> p (a b c)"), Cblk_d[l], None)
        V(ts(nCblk.rearrange("p a b c -> p (a b c)"), Cblk.rearrange("p a b c -> p (a b c)"), -1.0, ALU.mult), r=["Cblk"], w=["nCblk"])
        wload("Dblk", Dblk.rearrange("p a b -> p (a b)"), Dblk_d[l], None)
        t.dma("sp", "T1", lambda e: e.dma_start(out=T1, in_=abc_d[l, 0]), w=["T1"])
        t.dma("sp", "T2", lambda e: e.dma_start(out=T2, in_=abc_d[l, 1]), w=["T2"])
        t.dma("sp", "Wre", lambda e: e.dma_start(out=Wre, in_=abc_d[l, 2]), w=["Wre"])
        A(act(Wre, Wre, AF.Exp), r=["Wre"], w=["Wre"])
        zr, zi = zc[:, 0], zc[:, 1]
        V(tt(zr, T1, Wre, ALU.mult), r=["T1", "Wre"], w=["zc"])
        V(tt(zi, T2, Wre, ALU.mult), r=["T2", "Wre"], w=["zc"])
        PPf = PP.bitcast(F32).rearrange("p a b -> p (a b)")
        lbr, lbi = PPf[:, 0:1024], PPf[:, 1024:2048]
        sincos(lbi, lbr, zi, ["zc"], "PP")
        A(act(Wim, zr, AF.Exp), r=["zc"], w=["Wim"])
        V(tt(lbr, lbr, Wim, ALU.mult), r=["PP", "Wim"], w=["PP"])
        V(tt(lbi, lbi, Wim, ALU.mult), r=["PP", "Wim"], w=["PP"])
        cre, cim = Ere.rearrange("p k t -> p (k t)"), Eim.rearrange("p k t -> p (k t)")
        V(ts(lbr, lbr, -1.0, ALU.add), r=["PP"], w=["PP"])
        V(tt(T3, T1, T1, ALU.mult), r=["T1"], w=["T3"])
        V(tt(T4, T2, T2, ALU.mult), r=["T2"], w=["T4"])
        V(tt(T3, T3, T4, ALU.add), r=["T3", "T4"], w=["T3"])
        V(lambda e: e.reciprocal(out=T3, in_=T3), r=["T3"], w=["T3"])
        V(tt(cre, lbr, T1, ALU.mult), r=["PP", "T1"], w=["E", "PP"])
        V(tt(T4, lbi, T2, ALU.mult), r=["PP", "T2"], w=["T4"])
        V(tt(cre, cre, T4, ALU.add), r=["PP", "T4"], w=["E", "PP"])
        V(tt(cre, cre, T3, ALU.mult), r=["PP", "T3"], w=["E", "PP"])
        V(tt(cim, lbi, T1, ALU.mult), r=["PP", "T1"], w=["E", "PP"])
        V(tt(T4, lbr, T2, ALU.mult), r=["PP", "T2"], w=["T4"])
        V(tt(cim, cim, T4, ALU.subtract), r=["PP", "T4"], w=["E", "PP"])
        V(tt(cim, cim, T3, ALU.mult), r=["PP", "T3"], w=["E", "PP"])
        A(lambda e: e.activation(out=T1, in_=zr, func=AF.Exp, scale=ntcol), r=["zc", "misc"], w=["T1"])
        V(ts(T2, zi, ntcol, ALU.mult), r=["zc", "misc"], w=["T2"])
        sincos(lbi, lbr, T2, ["T2"], "PP")
        V(tt(lbr, lbr, T1, ALU.mult), r=["PP", "T1"], w=["PP"])
        V(tt(lbi, lbi, T1, ALU.mult), r=["PP", "T1"], w=["PP"])
        V(tt(Wre, cre, lbr, ALU.mult), r=["PP", "PP"], w=["E", "Wre"])
        V(tt(T4, cim, lbi, ALU.mult), r=["PP", "PP"], w=["E", "T4"])
        V(tt(Wre, Wre, T4, ALU.subtract), r=["Wre", "T4"], w=["Wre"])
        V(tt(Wim, cre, lbi, ALU.mult), r=["PP", "PP"], w=["E", "Wim"])
        V(tt(T4, cim, lbr, ALU.mult), r=["PP", "PP"], w=["E", "T4"])
        V(tt(Wim, Wim, T4, ALU.add), r=["Wim", "T4"], w=["Wim"])
        A(act(acol[:, 16:24], acol[:, 16:24], AF.Exp), r=["acol"], w=["acol"])
        V(tt(acol[:, 0:8], acol[:, 0:8], acol[:, 16:24], ALU.mult), r=["acol"], w=["acol"])
        V(tt(acol[:, 8:16], acol[:, 8:16], acol[:, 16:24], ALU.mult), r=["acol"], w=["acol"])
        T1v = T1.rearrange("p (k t) -> p k t", t=128)
        T2v = T2.rearrange("p (k t) -> p k t", t=128)
        for k in range(8):
            A(lambda e, k=k: e.activation(out=T1v[:, k], in_=tvals, func=AF.Exp, scale=acol[:, k:k + 1]), r=["tvals", "acol"], w=["T1"])
            V(ts(T2v[:, k], tvals, acol[:, 8 + k:9 + k], ALU.mult), r=["tvals", "acol"], w=["T2"])
        Ere2, Eim2 = Ere.rearrange("p k t -> p (k t)"), Eim.rearrange("p k t -> p (k t)")
        sincos(Eim2, Ere2, T2, ["T2"], "E")
        V(tt(Ere2, Ere2, T1, ALU.mult), r=["E", "T1"], w=["E"])
        V(tt(Eim2, Eim2, T1, ALU.mult), r=["E", "T1"], w=["E"])
        V(lambda e: e.memset(Xst.rearrange("p a b -> p (a b)"), 0.0), w=["Xst"])
        V(lambda e: e.memset(krb, 0.0), w=["krb"])
        V(lambda e: e.memset(Vaug[:, :, :, 64:65].rearrange("p n h o -> p (n h o)"), 1.0), w=["Vones"])
        V(lambda e: e.tensor_copy(out=Vaug[:, 0, :, 64:65].rearrange("p h o -> p (h o)"), in_=valid0.to_broadcast([128, 4])), r=["misc", "Vones"], w=["Vones"])

        for n in range(NB):
            blockM(l, n)

    def load_h(l, n, first_phase):
        h = hs[n % 2]
        hn = "h%d" % (n % 2)
        src = xin if (l == 0 and first_phase) else hbuf
        t.dma("sp", hn, lambda e: e.dma_start(out=h, in_=src[n * 128:(n + 1) * 128, :]), r=["hbufrow%d" % n], w=[hn])
        return h, hn

    def prenorm(h, hn, n, grp="A", g0=None):
        stat = stats[n % 2]
        sfx = "_%d" % (n % 2)
        A(act(sqj, h, AF.Square, accum=stat[:, 0:1]), r=[hn], w=["stat0" + sfx])
        rstd_pool(stat[:, 0:1], 1, [1.0 / 1024], stat[:, 1:2], "stat0" + sfx, "stat1" + sfx)
        V(ts(a_bf, h, stat[:, 1:2], ALU.mult), r=[hn, "stat1" + sfx], w=["a_bf"])
        bk, bn = nb(grp)
        bkb = bk.bitcast(BF16)
        for k in range(8):
            P(tp(bkb[:, k * 128:(k + 1) * 128], a_bf[:, k * 128:(k + 1) * 128]), r=["a_bf", "ident"], x=[bn])
        if g0 is None:
            A(act(aT.rearrange("p k t -> p (k t)"), bkb, AF.Copy), x=[bn], w=["aT"])
        else:
            for k in range(8):
                A(hint(lambda e, k=k: e.activation(out=aT[:, k, :], in_=bkb[:, k * 128:(k + 1) * 128], func=AF.Copy, scale=gcols[:, g0 + k:g0 + k + 1]), aT[:, k, :]), r=["gcols"], x=[bn], w=["aT"])

    def postnorm_store(l, n, b0, b0n, b1, b1n, h, hn, dst_rows):
        stat = stats[n % 2]
        sfx = "_%d" % (n % 2)
        A(act(sqj[:, 0:512], b0, AF.Square, accum=stat[:, 2:3]), x=[b0n], w=["stat2" + sfx])
        A(act(sqj[:, 512:1024], b1, AF.Square, accum=stat[:, 3:4]), x=[b1n], w=["stat2" + sfx])
        G(tt(stat[:, 4:5], stat[:, 2:3], stat[:, 3:4], ALU.add), r=["stat2" + sfx], w=["stat4" + sfx])
        rstd_pool(stat[:, 4:5], 1, [1.0 / 1024], stat[:, 5:6], "stat4" + sfx, "stat5" + sfx)
        ho = hnew[0]
        hon = "hnew0"
        V(lambda e: e.scalar_tensor_tensor(out=ho[:, 0:512], in0=b0, scalar=stat[:, 5:6], in1=gpost[:, 0:512], op0=ALU.mult, op1=ALU.mult), r=["stat5" + sfx, "gpost"], x=[b0n], w=[hon])
        V(lambda e: e.scalar_tensor_tensor(out=ho[:, 512:1024], in0=b1, scalar=stat[:, 5:6], in1=gpost[:, 512:1024], op0=ALU.mult, op1=ALU.mult), r=["stat5" + sfx, "gpost"], x=[b1n], w=[hon])
        G(tt(ho, ho, h, ALU.add), r=[hon, hn], w=[hon])
        dst = dst_rows
        t.dma("sp", hon, lambda e: e.dma_start(out=dst, in_=ho), r=[hon], w=["hbufrow%d" % n])

    def blockM(l, n):
        par = n % 2
        sfx = "_%d" % par
        stat, QT, yssm, oswa, omla, den, mixed, mixT, cqkv, krf = (stats[par], QTs[par], yssms[par], oswas[par], omlas[par],
                                                                     dens[par], mixeds[par], mixTs[par], cqkvs[par], krfs[par])
        h, hn = load_h(l, n, True)
        prenorm(h, hn, n, "E")
        sl = n % 2
        kT, vaug = kTs[sl], vaugs[sl]
        kTn, vaugn = "kT%d" % sl, "vaug%d" % sl
        bA, bAn = nb("E")
        for k in range(8):
            P(mm(bA[:, 0:480], aT[:, k, :], w_inT[:, k, :], k == 0, k == 7), r=["aT", "w_inT"], x=[bAn])
        bB, bBn = nb("E")
        bC, bCn = nb("E")
        for m in range(7):
            dstb = bB[:, m * 128:(m + 1) * 128] if m < 4 else bC[:, (m - 4) * 128:(m - 3) * 128]
            for k in range(8):
                P(mm(dstb, w_inF[:, k, m * 128:(m + 1) * 128], aT[:, k, :], k == 0, k == 7), r=["aT", "w_inF"], x=[bBn if m < 4 else bCn])
        V(lambda e: e.tensor_copy(out=vaug[:, :, 0:64], in_=bA[:, 0:128].rearrange("p (g d) -> p g d", g=2)), x=[bAn], w=[vaugn])
        V(lambda e: e.tensor_copy(out=vaug[:, :, 64:65].rearrange("p g o -> p (g o)"), in_=(valid0 if n == 0 else ones[:, 0:1]).to_broadcast([128, 2])), r=["misc", "ones"], w=[vaugn])
        A(act(sqj[:, 0:192], bA[:, 128:320], AF.Square, accum=stat[:, 6:7]), x=[bAn], w=["stat6" + sfx])
        A(act(sqj[:, 192:320], bA[:, 320:448], AF.Square, accum=stat[:, 7:8]), x=[bAn], w=["stat6" + sfx])
        V(lambda e: e.tensor_copy(out=cqkv, in_=bA[:, 128:448]), x=[bAn], w=["cqkv" + sfx])
        V(lambda e: e.tensor_copy(out=krf, in_=bA[:, 448:480]), x=[bAn], w=["krf" + sfx])
        V(lambda e: e.tensor_copy(out=qT.rearrange("p k t -> p (k t)"), in_=bB), x=[bBn], w=["qT"])
        A(act(uT.rearrange("p k t -> p (k t)"), bC[:, 0:256], AF.Copy), x=[bCn], w=["uT"])
        A(act(kT, bC[:, 256:384], AF.Copy), x=[bCn], w=[kTn])
        rstd_pool(stat[:, 6:8], 2, [1.0 / 192, 1.0 / 128], stat[:, 8:10], "stat6" + sfx, "stat8" + sfx)

        whs = [1] if n == 0 else [0, 1]
        for g in range(2):
            bo, bon = nb("E")
            ps_ = []
            for wi, which in enumerate(whs):
                ksl = (1 - sl) if which == 0 else sl
                bs, bsn = nb("E")
                P(mm(bs, kTs[ksl][64 * g:64 * g + 64, :], qT[64 * g:64 * g + 64, :, :].rearrange("p k t -> p (k t)"), True, True), r=["kT%d" % ksl, "qT"], x=[bsn])
                p_ = pS[g * 2 + which]
                pn = "pS%d" % (g * 2 + which)
                A(act(p_, bs, AF.Exp, scale=0.125), x=[bsn], w=[pn])
                V(tt(p_, p_, eswa[:, which, g * 512:(g + 1) * 512], ALU.mult), r=["eswa"], w=[pn])
                ps_.append((p_, pn, ksl, wi))
            bov = bo[:, 0:260].rearrange("p (h d) -> p h d", d=65)
            for hh in range(4):
                for (p_, pn, ksl, wi) in ps_:
                    P(mm(bov[:, hh, :], p_[:, hh * 128:(hh + 1) * 128], vaugs[ksl][:, g, :], wi == 0, wi == len(ps_) - 1), r=[pn, "vaug%d" % ksl], x=[bon])
            V(tt(den[:, 0:4], bov[:, :, 64], esink[:, g * 4:(g + 1) * 4], ALU.add), r=["esink"], x=[bon], w=["den" + sfx])
            V(lambda e: e.reciprocal(out=den[:, 0:4], in_=den[:, 0:4]), r=["den" + sfx], w=["den" + sfx])
            V(tt(oswa[:, g * 4:(g + 1) * 4, :], bov[:, :, 0:64], den[:, 0:4].unsqueeze(2).to_broadcast([128, 4, 64]), ALU.mult), r=["den" + sfx], x=[bon], w=["oswa" + sfx])
        A(act(sqj[:, 0:512], oswa.rearrange("p h d -> p (h d)"), AF.Square, accum=stat[:, 11:12]), r=["oswa" + sfx], w=["stat10" + sfx])

        V(ts(dq, ident, stat[:, 8:9], ALU.mult), r=["ident", "stat8" + sfx], w=["dq"])
        V(ts(dkv, ident, stat[:, 9:10], ALU.mult), r=["ident", "stat8" + sfx], w=["dkv"])
        b1, b1n = nb("E")
        P(mm(b1[:, 0:128], cqkv[:, 0:128], dq, True, True), r=["cqkv" + sfx, "dq"], x=[b1n])
        P(mm(b1[0:64, 128:256], cqkv[:, 128:192], dq, True, True), r=["cqkv" + sfx, "dq"], x=[b1n])
        P(mm(b1[:, 256:384], cqkv[:, 192:320], dkv, True, True), r=["cqkv" + sfx, "dkv"], x=[b1n])
        A(act(cqT[:, 0, :], b1[:, 0:128], AF.Copy), x=[b1n], w=["cqT"])
        A(act(cqT[0:64, 1, :], b1[0:64, 128:256], AF.Copy), x=[b1n], w=["cqT"])
        A(act(cT, b1[:, 256:384], AF.Copy), x=[b1n], w=["cT"])
        b2, b2n = nb("E")
        P(mm(b2[:, 0:384], cqT[:, 0, :], w_uq[:, 0, :], True, False), r=["cqT", "w_uq"], x=[b2n])
        P(mm(b2[:, 0:384], cqT[0:64, 1, :], w_uq[0:64, 1, :], False, True), r=["cqT", "w_uq"], x=[b2n])
        b3, b3n = nb("E")
        for hh in range(4):
            P(mm(b3[0:64, hh * 128:(hh + 1) * 128], w_uk[:, hh * 64:(hh + 1) * 64], cT, True, True), r=["cT", "w_uk"], x=[b3n])
        b4, b4n = nb("E")
        P(mm(b4[:, 0:256], cT, w_uv, True, True), r=["cT", "w_uv"], x=[b4n])
        KTn, Vn = "KT%d" % n, "V%d" % n
        ncol = slice(n * 128, (n + 1) * 128)
        V(lambda e: e.tensor_copy(out=KT[0:64, :, ncol], in_=b3[0:64, :].rearrange("p (h t) -> p h t", t=128)), x=[b3n], w=["T1", "T2", "T3", "T4", "TI", KTn])
        V(lambda e: e.tensor_copy(out=Vaug[:, n, :, 0:64], in_=b4[:, 0:256].rearrange("p (h d) -> p h d", d=64)), x=[b4n], w=[Vn])
        A(act(qf.rearrange("p h d -> p (h d)"), b2[:, 0:384], AF.Copy), x=[b2n], w=["qf"])
        cb = ropec[:, n:n + 1, :].to_broadcast([128, 4, 16])
        sb_ = ropes[:, n:n + 1, :].to_broadcast([128, 4, 16])
        x1, x2 = qf[:, :, 64:80], qf[:, :, 80:96]
        V(tt(rtmp[:, 0], x1, cb, ALU.mult), r=["qf", "ropec"], w=["rtmp"])
        V(tt(rtmp[:, 1], x2, sb_, ALU.mult), r=["qf", "ropes"], w=["rtmp"])
        V(tt(rtmp[:, 2], x1, sb_, ALU.mult), r=["qf", "ropes"], w=["rtmp"])
        V(tt(rtmp[:, 3], x2, cb, ALU.mult), r=["qf", "ropec"], w=["rtmp"])
        V(tt(qb[:, :, 64:80], rtmp[:, 0], rtmp[:, 1], ALU.subtract), r=["rtmp"], w=["qb"])
        V(tt(qb[:, :, 80:96], rtmp[:, 2], rtmp[:, 3], ALU.add), r=["rtmp"], w=["qb"])
        V(lambda e: e.tensor_copy(out=qb[:, :, 0:64], in_=qf[:, :, 0:64]), r=["qf"], w=["qb"])
        b5, b5n = nb("E")
        b5b = b5.bitcast(BF16)
        for hh in range(4):
            P(tp(b5b[0:96, hh * 128:(hh + 1) * 128], qb[:, hh, :]), r=["qb", "ident"], x=[b5n])
        A(act(QT[0:96].rearrange("p h t -> p (h t)"), b5b[0:96, 0:512], AF.Copy), x=[b5n], w=["QT" + sfx])
        c1, s1 = ropec[:, n, :], ropes[:, n, :]
        k1, k2 = krf[:, 0:16], krf[:, 16:32]
        rt = rtmp.rearrange("p a b c -> p (a b c)")
        V(tt(rt[:, 0:16], k1, c1, ALU.mult), r=["krf" + sfx, "ropec"], w=["rtmp"])
        V(tt(rt[:, 16:32], k2, s1, ALU.mult), r=["krf" + sfx, "ropes"], w=["rtmp"])
        V(tt(rt[:, 32:48], k1, s1, ALU.mult), r=["krf" + sfx, "ropes"], w=["rtmp"])
        V(tt(rt[:, 48:64], k2, c1, ALU.mult), r=["krf" + sfx, "ropec"], w=["rtmp"])
        V(tt(krb[:, 64:80], rt[:, 0:16], rt[:, 16:32], ALU.subtract), r=["rtmp"], w=["krb"])
        V(tt(krb[:, 80:96], rt[:, 32:48], rt[:, 48:64], ALU.add), r=["rtmp"], w=["krb"])
        b6, b6n = nb("E")
        b6b = b6.bitcast(BF16)
        P(tp(b6b[0:96, 0:128], krb), r=["krb", "ident"], x=[b6n])
        A(lambda e: e.activation(out=KT[64:96, :, ncol], in_=b6b[64:96, 0:128].unsqueeze(1).to_broadcast([32, 4, 128]), func=AF.Copy), x=[b6n], w=["T1", "T2", "T3", "T4", "TI", KTn])
        bo, bon = nb("L")
        bov = bo[:, 0:260].rearrange("p (h d) -> p h d", d=65)
        scale = 96.0 ** -0.5
        for j in range(n + 1):
            bs, bsn = nb("L")
            if bsn == bon:
                bs, bsn = nb("L")
            for hh in range(4):
                P(mm(bs[:, hh * 128:(hh + 1) * 128], KT[0:96, hh, j * 128:(j + 1) * 128], QT[0:96, hh, :], True, True), r=["KT%d" % j, "QT" + sfx], x=[bsn])
            p_ = pTs[j % 3]
            pn = "pT%d" % (j % 3)
            A(act(p_, bs, AF.Exp, scale=scale), x=[bsn], w=[pn])
            if j == n:
                V(tt(p_, p_, mask4, ALU.mult), r=["mask4"], w=[pn])
            for hh in range(4):
                P(mm(bov[:, hh, :], p_[:, hh * 128:(hh + 1) * 128], Vaug[:, j, hh, :], j == 0 and hh == 0, j == n and hh == 3), r=[pn, "V%d" % j, "Vones"], x=[bon])
        V(ts(den[:, 4:8], bov[:, :, 64], 1e-30, ALU.add), x=[bon], w=["den2" + sfx])
        V(lambda e: e.reciprocal(out=den[:, 4:8], in_=den[:, 4:8]), r=["den2" + sfx], w=["den2" + sfx])
        V(tt(omla, bov[:, :, 0:64], den[:, 4:8].unsqueeze(2).to_broadcast([128, 4, 64]), ALU.mult), r=["den2" + sfx], x=[bon], w=["omla" + sfx])
        A(act(sqj[:, 0:256], omla.rearrange("p h d -> p (h d)"), AF.Square, accum=stat[:, 12:13]), r=["omla" + sfx], w=["stat10" + sfx])

        bu = {}
        for j in range(2):
            for ri in range(2):
                bk, bn = nb("E")
                bu[(j, ri)] = (bk, bn)
                P(mm(bk, uT[:, j, :], Bblk[:, ri * 2 + j, :], True, True), r=["uT", "Bblk"], x=[bn])
        for j in range(2):
            sl_ = slice(j * 512, (j + 1) * 512)
            (br, brn), (bi, bin_) = bu[(j, 0)], bu[(j, 1)]
            V(tt(PP[:, 0, sl_], br, Wre[:, sl_], ALU.mult), r=["Wre"], x=[brn], w=["PP"])
            V(tt(PP[:, 3, sl_], br, Wim[:, sl_], ALU.mult), r=["Wim"], x=[brn], w=["PP"])
            V(tt(PP[:, 2, sl_], bi, Wre[:, sl_], ALU.mult), r=["Wre"], x=[bin_], w=["PP"])
            V(tt(PP[:, 1, sl_], bi, Wim[:, sl_], ALU.mult), r=["Wim"], x=[bin_], w=["PP"])
        lre, lim = Ere[:, :, 1], Eim[:, :, 1]
        V(tt(sm8[:, 0], lre, Xst[:, 0], ALU.mult), r=["E", "Xst"], w=["sm8"])
        V(tt(sm8[:, 1], lim, Xst[:, 1], ALU.mult), r=["E", "Xst"], w=["sm8"])
        V(tt(cst[:, 0], sm8[:, 0], sm8[:, 1], ALU.subtract), r=["sm8"], w=["cst"])
        V(tt(sm8[:, 2], lre, Xst[:, 1], ALU.mult), r=["E", "Xst"], w=["sm8"])
        V(tt(sm8[:, 3], lim, Xst[:, 0], ALU.mult), r=["E", "Xst"], w=["sm8"])
        V(tt(cst[:, 1], sm8[:, 2], sm8[:, 3], ALU.add), r=["sm8"], w=["cst"])
        zcv = zc.rearrange("p c (k t) -> p c k t", t=128)
        for hh in range(2):
            bzr, bzrn = nb("E")
            bzi, bzin = nb("E")
            for kk in range(4):
                k = hh * 4 + kk
                ks = slice(k * 128, (k + 1) * 128)
                o_ = slice(kk * 128, (kk + 1) * 128)
                P(mm(bzr[:, o_], PP[:, 0, ks], tri, True, False), r=["PP", "tri"], x=[bzrn])
                P(mm(bzr[:, o_], PP[:, 1, ks], ntri, False, True), r=["PP", "ntri"], x=[bzrn])
                P(mm(bzi[:, o_], PP[:, 2, ks], tri, True, False), r=["PP", "tri"], x=[bzin])
                P(mm(bzi[:, o_], PP[:, 3, ks], tri, False, True), r=["PP", "tri"], x=[bzin])
            V(tt(zcv[:, 0, hh * 4:(hh + 1) * 4, :], bzr.rearrange("p (k t) -> p k t", t=128), cst[:, 0, hh * 4:(hh + 1) * 4].unsqueeze(2).to_broadcast([128, 4, 128]), ALU.add), r=["cst"], x=[bzrn], w=["zc"])
            V(tt(zcv[:, 1, hh * 4:(hh + 1) * 4, :], bzi.rearrange("p (k t) -> p k t", t=128), cst[:, 1, hh * 4:(hh + 1) * 4].unsqueeze(2).to_broadcast([128, 4, 128]), ALU.add), r=["cst"], x=[bzin], w=["zc"])
        Ere2, Eim2 = Ere.rearrange("p k t -> p (k t)"), Eim.rearrange("p k t -> p (k t)")
        V(tt(QQ[:, 0], Ere2, zc[:, 0], ALU.mult), r=["E", "zc"], w=["PP"])
        V(tt(QQ[:, 1], Eim2, zc[:, 1], ALU.mult), r=["E", "zc"], w=["PP"])
        V(tt(QQ[:, 2], Ere2, zc[:, 1], ALU.mult), r=["E", "zc"], w=["PP"])
        V(tt(QQ[:, 3], Eim2, zc[:, 0], ALU.mult), r=["E", "zc"], w=["PP"])
        e7r, e7i = Ere[:, :, 127], Eim[:, :, 127]
        z7r, z7i = zcv[:, 0, :, 127], zcv[:, 1, :, 127]
        V(tt(sm8[:, 4], e7r, z7r, ALU.mult), r=["E", "zc"], w=["sm8b"])
        V(tt(sm8[:, 5], e7i, z7i, ALU.mult), r=["E", "zc"], w=["sm8b"])
        V(tt(sm8[:, 6], e7r, z7i, ALU.mult), r=["E", "zc"], w=["sm8b"])
        V(tt(sm8[:, 7], e7i, z7r, ALU.mult), r=["E", "zc"], w=["sm8b"])
        V(tt(Xst[:, 0], sm8[:, 4], sm8[:, 5], ALU.subtract), r=["sm8b"], w=["Xst"])
        V(tt(Xst[:, 1], sm8[:, 6], sm8[:, 7], ALU.add), r=["sm8b"], w=["Xst"])
        by, byn = nb("E")
        QQv = QQ.rearrange("p q (k t) -> p q k t", t=128)
        for k in range(8):
            o_ = by[:, k * 32:(k + 1) * 32]
            P(mm(o_, QQv[:, 0, k, :], Cblk[:, 0, k, :], True, False), r=["PP", "Cblk"], x=[byn])
            P(mm(o_, QQv[:, 1, k, :], nCblk[:, 0, k, :], False, False), r=["PP", "nCblk"], x=[byn])
            P(mm(o_, QQv[:, 2, k, :], nCblk[:, 1, k, :], False, False), r=["PP", "nCblk"], x=[byn])
            P(mm(o_, QQv[:, 3, k, :], nCblk[:, 1, k, :], False, False), r=["PP", "nCblk"], x=[byn])
            P(mm(o_, uT[:, k // 4, :], Dblk[:, k // 4, (k % 4) * 32:(k % 4) * 32 + 32], False, True), r=["uT", "Dblk"], x=[byn])
        A(act(ysb, by[:, 0:256], AF.Copy), x=[byn], w=["ysb"])
        V(tt(g1, ysb, ysb, ALU.mult), r=["ysb"], w=["g1"])
        V(ts(g1, g1, 0.044715, ALU.mult, 1.0, ALU.add), r=["g1"], w=["g1"])
        V(tt(g1, g1, ysb, ALU.mult), r=["g1", "ysb"], w=["g1"])
        A(act(g2, g1, AF.Tanh, scale=math.sqrt(2.0 / math.pi)), r=["g1"], w=["g2"])
        V(ts(g2, g2, 0.5, ALU.mult, 0.5, ALU.add), r=["g2"], w=["g2"])
        V(tt(zf, g2, ysb, ALU.mult), r=["g2", "ysb"], w=["zf"])
        V(lambda e: e.tensor_copy(out=zb, in_=zf), r=["zf"], w=["zb"])
        bk, bn = nb("E")
        bkb = bk.bitcast(BF16)
        for j in range(2):
            P(tp(bkb[:, j * 128:(j + 1) * 128], zb[:, j * 128:(j + 1) * 128]), r=["zb", "ident"], x=[bn])
        A(act(zT.rearrange("p k t -> p (k t)"), bkb[:, 0:256], AF.Copy), x=[bn], w=["zT"])
        bg, bgn = nb("E")
        for j in range(2):
            P(mm(bg[:, 0:256], zT[:, j, :], w_glu[:, j, :], j == 0, j == 1), r=["zT", "w_glu"], x=[bgn])
        V(tt(g1, bg[:, 0:256], bglu, ALU.add), r=["bglu"], x=[bgn], w=["g1"])
        A(act(g2, g1, AF.Tanh, scale=0.5), r=["g1"], w=["g2"])
        V(ts(g2, g2, 0.5, ALU.mult, 0.5, ALU.add), r=["g2"], w=["g2"])
        V(tt(yssm, g2, zf, ALU.mult), r=["g2", "zf"], w=["yssm" + sfx])
        A(act(sqj[:, 0:256], yssm, AF.Square, accum=stat[:, 10:11]), r=["yssm" + sfx], w=["stat10" + sfx])

        rstd_pool(stat[:, 10:13], 3, [1.0 / 256, 1.0 / 512, 1.0 / 256], stat[:, 13:16], "stat10" + sfx, "stat13" + sfx)
        V(ts(mixed[:, 0:256], yssm, stat[:, 13:14], ALU.mult), r=["yssm" + sfx, "stat13" + sfx], w=["mixed" + sfx])
        V(ts(mixed[:, 256:768], oswa.rearrange("p h d -> p (h d)"), stat[:, 14:15], ALU.mult), r=["oswa" + sfx, "stat13" + sfx], w=["mixed" + sfx])
        V(ts(mixed[:, 768:1024], omla.rearrange("p h d -> p (h d)"), stat[:, 15:16], ALU.mult), r=["omla" + sfx, "stat13" + sfx], w=["mixed" + sfx])
        if dbg and l == 0:
            t.dma("pool", "dbg", lambda e: e.dma_start(out=dbg_d[n], in_=mixed), r=["mixed" + sfx])
        bk, bn = nb("L")
        bkb = bk.bitcast(BF16)
        for k in range(8):
            P(tp(bkb[:, k * 128:(k + 1) * 128], mixed[:, k * 128:(k + 1) * 128]), r=["mixed" + sfx, "ident"], x=[bn])
        A(act(mixT.rearrange("p k t -> p (k t)"), bkb, AF.Copy), x=[bn], w=["mixT" + sfx])
        o0, o0n = nb("L")
        o1, o1n = nb("L")
        for half, (ob, obn) in enumerate(((o0, o0n), (o1, o1n))):
            for k in range(8):
                P(mm(ob, mixT[:, k, :], w_out[:, k, half * 512:(half + 1) * 512], k == 0, k == 7), r=["mixT" + sfx, "w_out"], x=[obn])
        postnorm_store(l, n, o0, o0n, o1, o1n, h, hn, hbuf[n * 128:(n + 1) * 128, :])

    def phaseF(l):
        t.dma("sp", "gcols", lambda e: e.dma_start(out=gcols, in_=gcols_d[l]), w=["gcols"])
        t.dma("sp", "gpost", lambda e: e.dma_start(out=gpost, in_=gpost_d[l, 1]), w=["gpost"])
        cast_engs = ["act", "dve", "act", "dve", "pool"]
        ci = 0
        for k in range(8):
            for half in range(2):
                sg = stg[ci % 3]
                sgn = "stg%d" % (ci % 3)
                t.dma("sp", sgn, lambda e, k=k, half=half, sg=sg: e.dma_start(out=sg, in_=w_up_d[l, k * 128:(k + 1) * 128, half * 2048:(half + 1) * 2048]), w=[sgn], nbytes=1 << 20)
                dst = w_up[:, k, half * 2048:(half + 1) * 2048]
                eng = cast_engs[ci % 5]
                gc = gcols[:, 16 + k:17 + k]
                if eng == "act":
                    A(hint(lambda e, dst=dst, sg=sg, gc=gc: e.activation(out=dst, in_=sg, func=AF.Copy, scale=gc), dst), r=[sgn, "gcols"], w=["w_up_%d_%d" % (k, half)])
                else:
                    t.op(eng, ts(dst, sg, gc, ALU.mult), r=[sgn, "gcols"], w=["w_up_%d_%d" % (k, half)])
                ci += 1
        for i in range(16):
            sg = stg[ci % 3]
            sgn = "stg%d" % (ci % 3)
            t.dma("sp", sgn, lambda e, i=i, sg=sg: e.dma_start(out=sg.rearrange("p (k c) -> p k c", k=2), in_=w_dn_d[l, i * 256:(i + 1) * 256, :].rearrange("(k p) c -> p k c", p=128)), w=[sgn], nbytes=1 << 20)
            dst = w_dn[:, 2 * i:2 * i + 2, :].rearrange("p k c -> p (k c)")
            eng = cast_engs[ci % 5]
            if eng == "act":
                A(act(dst, sg, AF.Copy), r=[sgn], w=["w_dn_%d" % i])
            else:
                t.op(eng, hint(lambda e, dst=dst, sg=sg: e.tensor_copy(out=dst, in_=sg), dst), r=[sgn], w=["w_dn_%d" % i])
            ci += 1
        last = (l == DEPTH - 1)
        for n in range(NB):
            if last and n == 0:
                continue
            h, hn = load_h(l, n, False)
            prenorm(h, hn, n, "A")
            for mg in range(8):
                bk, bn = nb()
                for mi in range(4):
                    m = mg * 4 + mi
                    for k in range(8):
                        P(mm(bk[:, mi * 128:(mi + 1) * 128], w_up[:, k, m * 128:(m + 1) * 128], aT[:, k, :], k == 0, k == 7), r=["aT"] + ["w_up_%d_%d" % (k, mg // 4)], x=[bn])
                r_ = rl[mg % 2]
                rn = "rl%d" % (mg % 2)
                A(act(r_, bk, AF.Relu), x=[bn], w=[rn])
                (V if mg % 2 == 0 else G)(tt(hT[:, mg * 4:(mg + 1) * 4, :].rearrange("p m t -> p (m t)"), r_, r_, ALU.mult), r=[rn], w=["hT%d" % mg])
            o0, o0n = nb()
            o1, o1n = nb()
            for half, (ob, obn) in enumerate(((o0, o0n), (o1, o1n))):
                for m in range(32):
                    P(mm(ob, hT[:, m, :], w_dn[:, m, half * 512:(half + 1) * 512], m == 0, m == 31), r=["hT%d" % (m // 4), "w_dn_%d" % (m // 2)], x=[obn])
            dst = out_d[(n - 1) * 128:n * 128, :] if last else hbuf[n * 128:(n + 1) * 128, :]
            postnorm_store(l, n, o0, o0n, o1, o1n, h, hn, dst)

    for l in range(DEPTH):
        phaseM(l)
        t.barrier()
        phaseF(l)
        t.barrier()
    t.final_wait("sp", ["hnew0"])
    if dbg:
        t.final_wait("pool", ["mixed_0", "mixed_1"])
    t.emit()
    return nc


def _consts(NB):
    ident = np.eye(128, dtype=np.float32)
    s = np.arange(128)
    tri = (s[:, None] <= s[None, :]).astype(np.float32)
    ones = np.ones((128, 128), np.float32)
    slopes = 2.0 ** (-8.0 * np.arange(1, 9, dtype=np.float32) / 8)
    key, q = s[:, None], s[None, :]
    e = np.zeros((128, 2, 8, 128), np.float32)
    for h in range(8):
        dprev = (q + 128 - key).astype(np.float32)
        e[:, 0, h, :] = np.where(key > q, np.exp(-slopes[h] * dprev), 0.0)
        dcur = (q - key).astype(np.float32)
        e[:, 1, h, :] = np.where(key <= q, np.exp(-slopes[h] * dcur), 0.0)
    mask4 = np.tile(tri[:, None, :], (1, 4, 1)).reshape(128, 512)
    cbf = np.concatenate([ident, tri, ones, e.reshape(128, 2048), mask4], axis=1).astype(np.float32)
    half = 16
    inv_freq = (10000.0 ** (-np.arange(half, dtype=np.float32) / half)).astype(np.float32)
    pos = (np.arange(NB * 128) - PAD).astype(np.float32)
    ang = (pos[:, None] * inv_freq[None, :]).astype(np.float32)
    c = np.cos(ang).astype(np.float32).reshape(NB, 128, 16).transpose(1, 0, 2).reshape(128, NB * 16)
    sn = np.sin(ang).astype(np.float32).reshape(NB, 128, 16).transpose(1, 0, 2).reshape(128, NB * 16)
    tv = np.tile(np.arange(128, dtype=np.float32)[None, :], (128, 1))
    misc = np.zeros((128, 8), np.float32)
    misc[:, 0] = np.arange(128)
    misc[:, 1] = -np.arange(128)
    misc[:, 2] = (np.arange(128) >= PAD).astype(np.float32)
    misc[:, 3] = -0.5
    cf32 = np.concatenate([c, sn, tv, misc], axis=1).astype(np.float32)
    return cbf, cf32


def _layout(inp, NB, DEPTH):
    f = lambda a: np.ascontiguousarray(np.asarray(a, dtype=np.float32))
    w_in = f(inp["w_in"])[:DEPTH]
    u, q, k, v, cq, ckv, kr = np.split(w_in, [256, 768, 896, 1024, 1216, 1344], axis=2)
    qh = q.reshape(DEPTH, 1024, 8, 64)
    qperm = np.concatenate([np.concatenate([qh[:, :, j], qh[:, :, j + 4]], axis=2) for j in range(4)], axis=2)
    d = {}
    d["w_inF"] = f(np.concatenate([qperm, u, k], axis=2))
    d["w_inT"] = f(np.concatenate([v, cq, ckv, kr], axis=2))
    d["w_out"] = f(inp["w_out"])[:DEPTH]
    d["w_up"] = f(inp["w_mlp_up"])[:DEPTH]
    d["w_dn"] = f(inp["w_mlp_down"])[:DEPTH]
    wuq = np.zeros((DEPTH, 256, 384), np.float32)
    wuq[:, :192] = f(inp["mla_w_uq"])[:DEPTH]
    d["w_uq"] = wuq
    d["w_uk"] = f(inp["mla_w_uk"])[:DEPTH]
    d["w_uv"] = f(inp["mla_w_uv"])[:DEPTH]
    d["w_glu"] = f(inp["ssm_w_glu"])[:DEPTH]
    col = lambda g: f(g)[:DEPTH].reshape(DEPTH, -1, 128).transpose(0, 2, 1)
    qn = np.zeros((DEPTH, 256), np.float32)
    qn[:, :192] = f(inp["mla_q_norm"])[:DEPTH]
    d["gcols"] = f(np.concatenate([col(inp["norm_pre_mix"]), col(inp["norm_heads"]), col(inp["norm_pre_mlp"]),
                                   qn.reshape(DEPTH, 2, 128).transpose(0, 2, 1), col(inp["mla_kv_norm"])], axis=2))
    bc = lambda a: np.broadcast_to(f(a)[:DEPTH][:, None, :], (DEPTH, 128, f(a).shape[-1]))
    d["gpost"] = f(np.stack([bc(inp["norm_post_mix"]), bc(inp["norm_post_mlp"])], axis=1))
    d["bglu"] = f(bc(inp["ssm_b_glu"]))
    d["sinks"] = f(bc(inp["swa_sinks"]))
    Bb = np.zeros((DEPTH, 128, 2, 2, 512), np.float32)
    for ri, nm in enumerate(("ssm_b_re", "ssm_b_im")):
        b = f(inp[nm])[:DEPTH]
        for j in range(2):
            for gl in range(8):
                Bb[:, gl * 16:(gl + 1) * 16, ri, j, gl * 64:(gl + 1) * 64] = b[:, 8 * j + gl].transpose(0, 2, 1)
    d["Bblk"] = f(Bb.reshape(DEPTH, 128, 2048))
    Cb = np.zeros((DEPTH, 128, 2, 8, 32), np.float32)
    for ri, nm in enumerate(("ssm_c_re", "ssm_c_im")):
        c = f(inp[nm])[:DEPTH]
        for k_ in range(8):
            for gi in range(2):
                Cb[:, gi * 64:(gi + 1) * 64, ri, k_, gi * 16:(gi + 1) * 16] = c[:, 2 * k_ + gi].transpose(0, 2, 1)
    d["Cblk"] = f(Cb.reshape(DEPTH, 128, 512))
    Db = np.zeros((DEPTH, 128, 2, 128), np.float32)
    dd = f(inp["ssm_d"])[:DEPTH]
    for j in range(2):
        for c_ in range(128):
            Db[:, c_, j, c_] = dd[:, 128 * j + c_]
    d["Dblk"] = f(Db.reshape(DEPTH, 128, 256))
    are = f(inp["ssm_a_re"])[:DEPTH].reshape(DEPTH, 1024)
    aim = f(inp["ssm_a_im"])[:DEPTH].reshape(DEPTH, 1024)
    ls = np.repeat(f(inp["ssm_log_step"])[:DEPTH], 64, axis=1)
    d["abc"] = f(np.stack([np.broadcast_to(a[:, None, :], (DEPTH, 128, 1024)) for a in (are, aim, ls)], axis=1))
    colS = lambda a: a.reshape(DEPTH, 8, 128).transpose(0, 2, 1)
    d["acol"] = f(np.concatenate([colS(are), colS(aim), colS(ls)], axis=2))
    cbf, cf32 = _consts(NB)
    d["cbf"], d["cf32"] = cbf, cf32
    return d


_CACHE = {}
DBG = False
DBG_OUT = {}


def kernel(**inputs):
    x = np.asarray(inputs["x"], dtype=np.float32)
    B, S, D = x.shape
    NB = S // 128 + 1
    DEPTH = np.asarray(inputs["w_in"]).shape[0]
    shared = _layout(inputs, NB, DEPTH)
    meta = np.asarray(inputs["meta_tokens"], dtype=np.float32)
    key = (NB, DEPTH)
    if key not in _CACHE:
        _CACHE[key] = build(NB, DEPTH, DBG)
    nc = _CACHE[key]
    in_maps = []
    for b in range(B):
        m = dict(shared)
        m["xin"] = np.ascontiguousarray(np.concatenate([np.zeros((PAD, D), np.float32), meta, x[b]], axis=0))
        in_maps.append(m)
    res = run_bass_kernel_spmd(nc, in_maps, core_ids=list(range(B)))
    if DBG:
        DBG_OUT["dbg"] = np.stack([np.asarray(r["dbg"]) for r in res.results], axis=0)
    return np.stack([np.asarray(r["out"], dtype=np.float32) for r in res.results], axis=0)
```
